# Optimizing a Trainium2 kernel written in Bass

```python
import math
import jax, jax.numpy as jnp
from jax import lax
import numpy as np

D_MODEL = 1024
BATCH = 4
SEQ = 4096
DEPTH = 1
DEC_BATCH = 8
DEC_SEQ = 2048
PAST_LEN = 128

POOL_WIDTH = 512
POOL_GROUPS = 4
POOL_GROUP_DIM = POOL_WIDTH // POOL_GROUPS
POOL_WINDOWS = (2, 4, 8, 16)
HYENA_WIDTH = 1024
HYENA_ORDER = 2
HYENA_IN = (HYENA_ORDER + 1) * HYENA_WIDTH
N_DIRS = 2
N_FILTER_SETS = N_DIRS * HYENA_ORDER
POS_BANDS = 16
POS_EMB = 1 + 2 * POS_BANDS
FILTER_HIDDEN = 64
DECAY_TARGET = 1e-2
FAST_DECAY_PCT = 0.3
SLOW_DECAY_PCT = 1.5
N_BRANCH = 2
IN_WIDTH = POOL_WIDTH + HYENA_IN + N_BRANCH * D_MODEL
D_FF = 4 * D_MODEL
DN_ALPHA = (2.0 * DEPTH) ** 0.25
DN_BETA = (8.0 * DEPTH) ** -0.25
LN_EPS = 1e-5
L1_EPS = 1e-6

kernel_name = 'hybrid_pool_hyena_deepnorm_encoder'


def layer_norm(x, g, b):
    xf = x.astype(jnp.float32)
    mu = jnp.mean(xf, axis=-1, keepdims=True)
    xc = xf - mu
    var = jnp.mean(xc * xc, axis=-1, keepdims=True)
    y = xc * lax.rsqrt(var + LN_EPS) * g.astype(jnp.float32) + b.astype(jnp.float32)
    return y.astype(x.dtype)


def centred_mean_minus_self(a):
    B, L, C = a.shape
    af = a.astype(jnp.float32)
    cs = jnp.concatenate([jnp.zeros((B, 1, C), jnp.float32), jnp.cumsum(af, axis=1)], axis=1)
    pos = jnp.arange(L)
    means = []
    for g, w in enumerate(POOL_WINDOWS):
        lo = jnp.clip(pos - w // 2, 0, L)
        hi = jnp.clip(pos + (w - w // 2), 0, L)
        csg = cs[:, :, g * POOL_GROUP_DIM:(g + 1) * POOL_GROUP_DIM]
        cnt = (hi - lo).astype(jnp.float32)[None, :, None]
        means.append((csg[:, hi] - csg[:, lo]) / cnt)
    return (jnp.concatenate(means, axis=-1) - af).astype(a.dtype)


def pool_branch(a, w_pool, b_pool, pool_scale, w_pool_proj):
    B, L, _ = a.shape
    p = centred_mean_minus_self(a).reshape(B, L, POOL_GROUPS, POOL_GROUP_DIM)
    p = jnp.einsum('blgc,gcd->blgd', p, w_pool) + b_pool
    p = p.reshape(B, L, POOL_WIDTH) * pool_scale
    return p @ w_pool_proj


def positional_features(L):
    t = jnp.linspace(0.0, 1.0, L, dtype=jnp.float32)[:, None]
    w = (2.0 * math.pi / L) * jnp.arange(L, dtype=jnp.float32)[:, None]
    f = jnp.linspace(1e-4, POS_BANDS - 1, POS_BANDS, dtype=jnp.float32)[None, :]
    z = jnp.concatenate([t, jnp.cos(f * w), -jnp.sin(f * w)], axis=-1)
    return z, t


def hyena_filter_spectra(L, w_f1, b_f1, freq_f1, w_f2, b_f2, freq_f2, w_f_out, decay_rate):
    f32 = jnp.float32
    z, t = positional_features(L)
    h = jnp.sin(freq_f1.astype(f32) * (z @ w_f1.astype(f32) + b_f1.astype(f32)))
    h = jnp.sin(freq_f2.astype(f32) * (h @ w_f2.astype(f32) + b_f2.astype(f32)))
    h = (h @ w_f_out.astype(f32)).reshape(L, N_FILTER_SETS, HYENA_WIDTH)
    h = h * jnp.exp(-t[:, :, None] * jnp.abs(decay_rate.astype(f32))[None])
    h = h.reshape(L, HYENA_ORDER, N_DIRS, HYENA_WIDTH)
    lag_pos = (jnp.arange(L) > 0).astype(f32)
    dir_mask = jnp.stack([jnp.ones((L,), f32), lag_pos], axis=1)[:, None, :, None]
    h = h * dir_mask
    h = h / (jnp.sum(jnp.abs(h), axis=(0, 2), keepdims=True) + L1_EPS)
    k = jnp.concatenate([h[:, :, 0], jnp.zeros((1, HYENA_ORDER, HYENA_WIDTH), f32), h[:0:-1, :, 1]], axis=0)
    return jnp.fft.rfft(k, axis=0)


def fft_long_conv(u, kf, bias):
    L = u.shape[1]
    uf = u.astype(jnp.float32)
    U = jnp.fft.rfft(uf, n=2 * L, axis=1)
    y = jnp.fft.irfft(U * kf[None], n=2 * L, axis=1)[:, :L]
    return (y + uf * bias.astype(jnp.float32)).astype(u.dtype)


def short_conv_centred(u, w, b):
    up = jnp.pad(u, ((0, 0), (1, 1), (0, 0)))
    return up[:, :-2] * w[0] + up[:, 1:-1] * w[1] + up[:, 2:] * w[2] + b


def hyena_branch(u3, conv_w, conv_b, kf, hyena_bias, w_hyena_proj):
    uc = short_conv_centred(u3, conv_w, conv_b)
    x1 = uc[..., :HYENA_WIDTH]
    x2 = uc[..., HYENA_WIDTH:2 * HYENA_WIDTH]
    z = uc[..., 2 * HYENA_WIDTH:]
    for o, gate in enumerate((x1, x2)):
        z = gate * fft_long_conv(z, kf[:, o], hyena_bias[o])
    return z @ w_hyena_proj


def token_mixer(h, p, l):
    B, L, _ = h.shape
    proj = h @ p['w_in'][l] + p['b_in'][l]
    a = proj[..., :POOL_WIDTH]
    u3 = proj[..., POOL_WIDTH:POOL_WIDTH + HYENA_IN]
    gl = proj[..., POOL_WIDTH + HYENA_IN:]
    ya = pool_branch(a, p['w_pool'][l], p['b_pool'][l], p['pool_scale'][l], p['w_pool_proj'][l])
    kf = hyena_filter_spectra(L, p['w_f1'][l], p['b_f1'][l], p['freq_f1'][l], p['w_f2'][l], p['b_f2'][l],
                              p['freq_f2'][l], p['w_f_out'][l], p['decay_rate'][l])
    yb = hyena_branch(u3, p['conv_w'][l], p['conv_b'][l], kf, p['hyena_bias'][l], p['w_hyena_proj'][l])
    g = jax.nn.sigmoid(gl.astype(jnp.float32)).astype(h.dtype).reshape(B, L, N_BRANCH, D_MODEL)
    m = g[:, :, 0] * ya + g[:, :, 1] * yb
    return m @ p['w_o'][l] + p['b_o'][l]


def channel_mixer(h, p, l):
    u = jax.nn.relu(h @ p['w_ff1'][l] + p['b_ff1'][l])
    return (u * u) @ p['w_ff2'][l] + p['b_ff2'][l]


def encoder_trunk(x, p):
    h = layer_norm(x, p['ln_in_g'], p['ln_in_b'])
    for l in range(DEPTH):
        h = layer_norm(DN_ALPHA * h + token_mixer(h, p, l), p['ln1_g'][l], p['ln1_b'][l])
        h = layer_norm(DN_ALPHA * h + channel_mixer(h, p, l), p['ln2_g'][l], p['ln2_b'][l])
    return h


def setup_inputs(seed: int = 0) -> dict:
    key = jax.random.key(seed)
    ks = jax.random.split(key, 40)
    f32 = jnp.float32

    def nrm(k, shape, scale):
        return jax.random.normal(k, shape, f32) * scale

    Dp = DEPTH
    min_decay = math.log(DECAY_TARGET) / SLOW_DECAY_PCT
    max_decay = math.log(DECAY_TARGET) / FAST_DECAY_PCT
    decay_base = jnp.linspace(min_decay, max_decay, HYENA_WIDTH, dtype=f32)
    dshape = (Dp, N_FILTER_SETS, HYENA_WIDTH)
    return {
        'x_prompt': nrm(ks[0], (BATCH, SEQ, D_MODEL), 1.0),
        'x_sample': nrm(ks[1], (DEC_BATCH, DEC_SEQ, D_MODEL), 1.0),
        'ln_in_g': 1.0 + nrm(ks[2], (D_MODEL,), 0.01),
        'ln_in_b': nrm(ks[3], (D_MODEL,), 0.01),
        'w_in': nrm(ks[4], (Dp, D_MODEL, IN_WIDTH), D_MODEL ** -0.5),
        'b_in': nrm(ks[5], (Dp, IN_WIDTH), 0.01),
        'w_pool': nrm(ks[6], (Dp, POOL_GROUPS, POOL_GROUP_DIM, POOL_GROUP_DIM), POOL_GROUP_DIM ** -0.5),
        'b_pool': nrm(ks[7], (Dp, POOL_GROUPS, POOL_GROUP_DIM), 0.01),
        'pool_scale': 1.0 + nrm(ks[8], (Dp, POOL_WIDTH), 0.1),
        'w_pool_proj': nrm(ks[9], (Dp, POOL_WIDTH, D_MODEL), POOL_WIDTH ** -0.5),
        'conv_w': nrm(ks[10], (Dp, 3, HYENA_IN), 3.0 ** -0.5),
        'conv_b': nrm(ks[11], (Dp, HYENA_IN), 0.01),
        'w_f1': nrm(ks[12], (Dp, POS_EMB, FILTER_HIDDEN), POS_EMB ** -0.5),
        'b_f1': nrm(ks[13], (Dp, FILTER_HIDDEN), 0.1),
        'freq_f1': 1.0 + nrm(ks[14], (Dp, FILTER_HIDDEN), 0.1),
        'w_f2': nrm(ks[15], (Dp, FILTER_HIDDEN, FILTER_HIDDEN), FILTER_HIDDEN ** -0.5),
        'b_f2': nrm(ks[16], (Dp, FILTER_HIDDEN), 0.1),
        'freq_f2': 1.0 + nrm(ks[17], (Dp, FILTER_HIDDEN), 0.1),
        'w_f_out': nrm(ks[18], (Dp, FILTER_HIDDEN, N_FILTER_SETS * HYENA_WIDTH), FILTER_HIDDEN ** -0.5),
        'decay_rate': jnp.broadcast_to(decay_base, dshape) + nrm(ks[19], dshape, 0.1),
        'hyena_bias': nrm(ks[20], (Dp, HYENA_ORDER, HYENA_WIDTH), 0.1),
        'w_hyena_proj': nrm(ks[21], (Dp, HYENA_WIDTH, D_MODEL), HYENA_WIDTH ** -0.5),
        'w_o': nrm(ks[22], (Dp, D_MODEL, D_MODEL), D_MODEL ** -0.5 * DN_BETA),
        'b_o': nrm(ks[23], (Dp, D_MODEL), 0.01),
        'ln1_g': 1.0 + nrm(ks[24], (Dp, D_MODEL), 0.01),
        'ln1_b': nrm(ks[25], (Dp, D_MODEL), 0.01),
        'w_ff1': nrm(ks[26], (Dp, D_MODEL, D_FF), D_MODEL ** -0.5),
        'b_ff1': nrm(ks[27], (Dp, D_FF), 0.01),
        'w_ff2': nrm(ks[28], (Dp, D_FF, D_MODEL), D_FF ** -0.5 * DN_BETA),
        'b_ff2': nrm(ks[29], (Dp, D_MODEL), 0.01),
        'ln2_g': 1.0 + nrm(ks[30], (Dp, D_MODEL), 0.01),
        'ln2_b': nrm(ks[31], (Dp, D_MODEL), 0.01),
    }


def reference(x_prompt, x_sample, ln_in_g, ln_in_b, w_in, b_in, w_pool, b_pool, pool_scale, w_pool_proj,
              conv_w, conv_b, w_f1, b_f1, freq_f1, w_f2, b_f2, freq_f2, w_f_out, decay_rate, hyena_bias,
              w_hyena_proj, w_o, b_o, ln1_g, ln1_b, w_ff1, b_ff1, w_ff2, b_ff2, ln2_g, ln2_b):
    params = {
        'ln_in_g': ln_in_g, 'ln_in_b': ln_in_b, 'w_in': w_in, 'b_in': b_in,
        'w_pool': w_pool, 'b_pool': b_pool, 'pool_scale': pool_scale, 'w_pool_proj': w_pool_proj,
        'conv_w': conv_w, 'conv_b': conv_b, 'w_f1': w_f1, 'b_f1': b_f1, 'freq_f1': freq_f1,
        'w_f2': w_f2, 'b_f2': b_f2, 'freq_f2': freq_f2, 'w_f_out': w_f_out, 'decay_rate': decay_rate,
        'hyena_bias': hyena_bias, 'w_hyena_proj': w_hyena_proj, 'w_o': w_o, 'b_o': b_o,
        'ln1_g': ln1_g, 'ln1_b': ln1_b, 'w_ff1': w_ff1, 'b_ff1': b_ff1, 'w_ff2': w_ff2, 'b_ff2': b_ff2,
        'ln2_g': ln2_g, 'ln2_b': ln2_b,
    }
    y_prompt = encoder_trunk(x_prompt, params)
    y_sample = encoder_trunk(x_sample, params)
    return (y_prompt, y_sample)
```

```python
import math
from contextlib import ExitStack

import numpy as np
import ml_dtypes

import concourse.bass as bass
import concourse.mybir as mybir
from concourse.bass_utils import run_bass_kernel_spmd

F32 = mybir.dt.float32
BF16 = mybir.dt.bfloat16
AF = mybir.ActivationFunctionType
ALU = mybir.AluOpType
NPBF = ml_dtypes.bfloat16

D = 1024
T = 4096
NSUB = 32
HW = 1024
DFF = 4096
LN_EPS = 1e-5
ALPHA = 2.0 ** 0.25
MAGIC = float(1.5 * 2 ** 23)
TWO_PI = float(2 * np.pi)
POOL_WINDOWS = (2, 4, 8, 16)


class Buf:
    __slots__ = ("name", "w", "r", "excl", "const", "al")

    def __init__(self, name, excl=False, const=False):
        self.name = name
        self.w = None
        self.r = []
        self.excl = excl
        self.const = const
        self.al = []


class Prog:
    ENGS = ("pe", "act", "dve", "pool", "sp")

    def __init__(self, nc, es):
        self.nc = nc
        self.es = es
        self.items = {k: [] for k in self.ENGS}
        self.cnt = {k: 0 for k in self.ENGS}
        self.seen = {k: {} for k in self.ENGS}
        self.sems = {}
        for k in self.ENGS:
            self.sems[k] = es.enter_context(nc.semaphore("s_" + k))
        self.dma_cnt = {}

    def _waits(self, eng, reads, writes):
        need = {}

        def add(ev, rr=False):
            if ev is None:
                return
            k, v = ev
            if k == eng and (eng == "pe" or rr):
                return
            if need.get(k, 0) < v:
                need[k] = v
        for b in reads:
            add(b.w)
            if b.excl:
                for ev in b.r:
                    add(ev, rr=True)
        for b in writes:
            add(b.w)
            for ev in b.r:
                add(ev)
            for a in b.al:
                add(a.w)
                for ev in a.r:
                    add(ev)
        out = []
        seen = self.seen[eng]
        for k, v in need.items():
            if seen.get(k, 0) < v:
                seen[k] = v
                out.append((k, v))
        return out

    def _record(self, ev, reads, writes):
        for b in reads:
            if not b.const:
                b.r.append(ev)
        for b in writes:
            b.w = ev
            b.r = []

    def op(self, eng, fn, reads=(), writes=()):
        waits = self._waits(eng, reads, writes)
        self.cnt[eng] += 1
        ev = (eng, self.cnt[eng])
        self.items[eng].append((waits, fn, (eng, 1)))
        self._record(ev, reads, writes)
        return ev

    def dma(self, eng, fn, semkey=None, reads=(), writes=()):
        if eng == "pool":
            semkey = "dq_%d" % len(self.dma_cnt)
        elif writes:
            semkey = "d_" + writes[0].name
        else:
            semkey = "d_" + reads[0].name
        if semkey not in self.sems:
            self.sems[semkey] = self.es.enter_context(self.nc.semaphore("d%d" % len(self.sems)))
            self.dma_cnt[semkey] = 0
        waits = self._waits(eng, reads, writes)
        self.dma_cnt[semkey] += 16
        ev = (semkey, self.dma_cnt[semkey])
        self.items[eng].append((waits, fn, (semkey, 16)))
        self._record(ev, reads, writes)
        return ev

    def final_wait(self, eng, bufs):
        waits = self._waits(eng, bufs, bufs)
        self.items[eng].append((waits, None, None))

    def emit(self):
        nc = self.nc
        sems = self.sems
        items = self.items
        with nc.Block() as block:
            def run(e, lst):
                for waits, fn, inc in lst:
                    for k, v in waits:
                        e.wait_ge(sems[k], v)
                    if fn is not None:
                        fn(e).then_inc(sems[inc[0]], inc[1])

            @block.sync
            def _(e):
                run(e, items["sp"])

            @block.tensor
            def _(e):
                run(e, items["pe"])

            @block.scalar
            def _(e):
                run(e, items["act"])

            @block.vector
            def _(e):
                run(e, items["dve"])

            @block.gpsimd
            def _(e):
                run(e, items["pool"])


def conv_consts(L, nseq):
    Aq = L // 128
    G = 2 * Aq
    b = np.arange(128)[:, None]
    f = np.arange(128)[None, :]
    ph = np.pi * (2 * f + 1) * b / 256.0
    Fb = np.concatenate([np.cos(ph), -np.sin(ph)], axis=1)
    phn = np.pi * (2 * f + 1) * (b - 128) / 256.0
    Fneg = np.concatenate([np.cos(phn), -np.sin(phn)], axis=1)
    Fneg[0, :] = 0.0
    SelR = np.zeros((4, 128, 128)); SelI = np.zeros((4, 128, 128))
    g = np.arange(G)
    for j in range(4):
        for a in range(32):
            s, ap = divmod(a, Aq)
            th = 2 * np.pi * g * ap / G
            row = j * 32 + a
            SelR[j, row, s * G + g] = np.cos(th)
            SelR[j, row, 64 + s * G + g] = -np.sin(th)
            SelI[j, row, s * G + g] = np.sin(th)
            SelI[j, row, 64 + s * G + g] = np.cos(th)
    T1 = np.zeros((128, 64)); T2 = np.zeros((128, 64))
    for s in range(nseq):
        for gg in range(G):
            for ap in range(Aq):
                a = s * Aq + ap
                th = 2 * np.pi * gg * ap / G
                Gr = np.cos(th) / G; Gi = np.sin(th) / G
                top = s * G + gg; bot = 64 + s * G + gg
                T1[top, a] = Gr;   T1[top, 32 + a] = Gi
                T1[bot, a] = -Gi;  T1[bot, 32 + a] = Gr
                T2[top, a] = -Gi;  T2[top, 32 + a] = Gr
                T2[bot, a] = -Gr;  T2[bot, 32 + a] = -Gi
    ff = np.arange(128)[:, None]; bb = np.arange(128)[None, :]
    ph2 = np.pi * (2 * ff + 1) * bb / 256.0
    Cb = np.cos(ph2) / 128.0
    Sbn = -np.sin(ph2) / 128.0
    FK = np.zeros((4, 2, 128, 128))
    for j in range(2):
        for r in range(64):
            d = r - 32
            if abs(d) > Aq - 1:
                continue
            th = 2 * np.pi * g * d / G
            row = j * 64 + r
            for ro in range(2):
                for s in range(nseq):
                    cols = ro * 64 + s * G + g
                    FK[0, j, row, cols] = np.cos(th)
                    FK[1, j, row, cols] = np.sin(th)
                    FK[2, j, row, cols] = -np.sin(th)
                    FK[3, j, row, cols] = np.cos(th)
    return dict(Fb=Fb, Fneg=Fneg, SelR=SelR, SelI=SelI, T1=T1, T2=T2, Cb=Cb, Sbn=Sbn, FK=FK)


def lag_table(L):
    p = np.arange(128)[None, :]
    r = np.arange(64)[:, None]
    lag = np.where(r < 32, 128 * (32 - r) - p, 128 * (r - 32) + p)
    valid = np.where(r < 32, (lag >= 1) & (lag <= L - 1), lag <= L - 1)
    return lag, valid


def pool_mats(L, nseq):
    def dense(g, n):
        w = POOL_WINDOWS[g]
        M = np.zeros((n, n))
        for t in range(n):
            lo = max(t - w // 2, 0); hi = min(t + (w - w // 2), n)
            M[t, lo:hi] = 1.0 / (hi - lo)
            M[t, t] -= 1.0
        return M.T
    tab = np.zeros((9, 4, 128, 128))
    for g in range(4):
        Mt = dense(g, 512)
        Dmid = Mt[128:256, 128:256]; Dfirst = Mt[0:128, 0:128]; Dlast = Mt[384:512, 384:512]
        Pmid = Mt[0:128, 128:256]
        Nmid = Mt[256:384, 128:256]
        tab[0, g] = Dmid; tab[1, g] = Dfirst; tab[2, g] = Dlast
        if nseq == 1:
            tab[3, g] = Dmid; tab[4, g] = Dmid; tab[6, g] = Pmid; tab[8, g] = Nmid
        else:
            tab[3, g] = Dlast; tab[4, g] = Dfirst
        tab[5, g] = Pmid; tab[7, g] = Nmid
    return tab


_CB = {}
_off = 0
for _n, _w in (("Fb", 256), ("Fneg", 256), ("SelR", 512), ("SelI", 512), ("T1", 64), ("T2", 64),
               ("Cb", 128), ("Sbn", 128), ("FK", 1024), ("ident", 128), ("ones", 128), ("pool", 36 * 128)):
    _CB[_n] = (_off, _w)
    _off += _w
NCB = _off


def host_consts(L, nseq):
    K = conv_consts(L, nseq)
    cb = np.zeros((128, NCB), np.float64)

    def put(name, arr):
        o, w = _CB[name]
        assert arr.shape == (128, w), (name, arr.shape)
        cb[:, o:o + w] = arr
    put("Fb", K["Fb"]); put("Fneg", K["Fneg"])
    put("SelR", np.concatenate(list(K["SelR"]), axis=1))
    put("SelI", np.concatenate(list(K["SelI"]), axis=1))
    put("T1", K["T1"]); put("T2", K["T2"]); put("Cb", K["Cb"]); put("Sbn", K["Sbn"])
    put("FK", np.concatenate([K["FK"][k, j] for k in range(4) for j in range(2)], axis=1))
    put("ident", np.eye(128)); put("ones", np.ones((128, 128)))
    pm = pool_mats(L, nseq)
    put("pool", np.concatenate([pm[k, g] for k in range(9) for g in range(4)], axis=1))
    lag, valid = lag_table(L)
    lagc = np.clip(lag, 0, L - 1).astype(np.float64)
    t = lagc / (L - 1)
    w = (2.0 * np.pi / L) * lagc
    fb = np.linspace(1e-4, 15.0, 16)
    z = np.concatenate([t[..., None], np.cos(fb * w[..., None]), -np.sin(fb * w[..., None])], axis=-1)
    z = z * valid[..., None]
    zT = z.reshape(64 * 128, 33).T.copy()
    vmask = valid.reshape(1, 8192).astype(np.float32).astype(NPBF)
    negt = (-(t * valid)).T.copy()
    gm = np.full((128, 1), 1.0 if nseq == 1 else 0.0, np.float32)
    return dict(cb16=cb.astype(np.float32), zT=zT.astype(np.float32), vmask=vmask,
                negt=negt.astype(np.float32), gm=gm)


def MM(out, lhsT, rhs, start=True, stop=True):
    return lambda e: e.matmul(out, lhsT=lhsT, rhs=rhs, start=start, stop=stop)


def TR(out, in_, ident):
    return lambda e: e.transpose(out, in_, ident)


def ACT(out, in_, func, bias=None, scale=None):
    kw = {}
    if bias is not None:
        kw["bias"] = bias
    if scale is not None:
        kw["scale"] = scale
    return lambda e: e.activation(out=out, in_=in_, func=func, **kw)


def TT(out, in0, in1, op):
    return lambda e: e.tensor_tensor(out=out, in0=in0, in1=in1, op=op)


def TS(out, in0, s1, op0, s2=None, op1=None):
    if op1 is None:
        return lambda e: e.tensor_scalar(out=out, in0=in0, scalar1=s1, scalar2=None, op0=op0)
    return lambda e: e.tensor_scalar(out=out, in0=in0, scalar1=s1, scalar2=s2, op0=op0, op1=op1)


def STT(out, in0, scalar, in1, op0, op1):
    return lambda e: e.scalar_tensor_tensor(out=out, in0=in0, scalar=scalar, in1=in1, op0=op0, op1=op1)


def CP(out, in_):
    return lambda e: e.tensor_copy(out=out, in_=in_)


def MSET(ap, v):
    return lambda e: e.memset(ap, v)


def DMA(out, in_, slow=False):
    if slow:
        return lambda e: e.dma_start(out=out, in_=in_, allow_slow_non_contiguous=True)
    return lambda e: e.dma_start(out=out, in_=in_)


class Arena:
    def __init__(self, t, nwords):
        self.t = t
        self.n = nwords
        self.off = 0

    def f32(self, cols, parts=128):
        assert self.off + cols <= self.n, ("arena overflow", self.off, cols, self.n)
        ap = self.t[0:parts, self.off:self.off + cols]
        self.off += cols
        return ap

    def bf16(self, cols, parts=128):
        w = (cols + 1) // 2
        assert self.off + w <= self.n, ("arena overflow", self.off, w, self.n)
        ap = self.t[0:parts, self.off:self.off + w].bitcast(BF16)
        self.off += w
        return ap[:, 0:cols]


class PsPool:
    def __init__(self, tens):
        self.t = tens
        self.b = [Buf("ps%d" % i, excl=True) for i in range(len(tens))]
        self.busy = [False] * len(tens)
        self.nxt = 0

    def get(self):
        n = len(self.t)
        for k in range(n):
            i = (self.nxt + k) % n
            if not self.busy[i]:
                self.busy[i] = True
                self.nxt = (i + 1) % n
                return i
        raise RuntimeError("no free psum bank")

    def rel(self, i):
        self.busy[i] = False


def _barrier(P):
    targets = [(k, P.cnt[k]) for k in P.ENGS if P.cnt[k] > 0]
    targets += [(k, v) for k, v in P.dma_cnt.items() if v > 0]
    for eng in P.ENGS:
        waits = []
        for k, v in targets:
            if k == eng:
                continue
            if P.seen[eng].get(k, 0) < v:
                P.seen[eng][k] = v
                waits.append((k, v))
        P.items[eng].append((waits, None, None))


INPUT_SPECS = [
    ("x", [T, D], F32),
    ("w_in", [D, 5632], F32), ("b_in", [1, 5632], F32),
    ("w_pool", [512, 128], F32), ("b_pool", [1, 512], F32), ("pool_scale", [1, 512], F32),
    ("w_pool_proj", [512, D], F32),
    ("conv_w", [3, 3072], F32), ("conv_b", [1, 3072], F32),
    ("w_f1", [33, 64], F32), ("b_f1", [64, 1], F32), ("freq_f1", [64, 1], F32),
    ("w_f2", [64, 64], F32), ("b_f2", [64, 1], F32), ("freq_f2", [64, 1], F32),
    ("w_f_out", [64, 4096], F32), ("decay_rate", [4, HW], F32), ("hyena_bias", [1, 2 * HW], F32),
    ("w_hyena_proj", [HW, D], F32), ("w_o", [D, D], F32), ("b_o", [1, D], F32),
    ("ln1_g", [1, D], F32), ("ln1_b", [1, D], F32),
    ("w_ff1", [D, DFF], F32), ("b_ff1", [1, DFF], F32), ("w_ff2", [DFF, D], F32), ("b_ff2", [1, D], F32),
    ("ln2_g", [1, D], F32), ("ln2_b", [1, D], F32), ("ln_in_g", [1, D], F32), ("ln_in_b", [1, D], F32),
    ("cb16", [128, NCB], F32), ("zT", [33, 8192], F32), ("vmask", [1, 8192], BF16),
    ("negt", [128, 64], F32), ("gm", [128, 1], F32),
]
NCA = _CB["pool"][0]
ARENA_WORDS = 34400


def build_program(ngroups=8, ntiles=8, dbg=False, phase2=True):
    nc = bass.Bass("TRN2", target_bir_lowering=False)
    I = {}
    for name, shape, dt in INPUT_SPECS:
        I[name] = nc.dram_tensor(name, list(shape), dt, kind="ExternalInput").ap()
    y_out = nc.dram_tensor("y", [T, D], F32, kind="ExternalOutput").ap()
    z2T_d = nc.dram_tensor("z2T_d", [HW, T], BF16, kind=("ExternalOutput" if dbg else "Internal")).ap()
    scr = {}
    for name, shape in (("wa_s", [D, 512]), ("wgt_s", [16, 128, 8, 128]), ("wpool_s", [512, 128]),
                        ("wpp_s", [8, 128, 4, 128]), ("whp_s", [8, 128, 8, 128]), ("wo_s", [D, D]),
                        ("wf1_s", [32, 128, 8, 128]), ("wf2_s", [DFF, D])):
        scr[name] = nc.dram_tensor(name, shape, BF16, kind="Internal").ap()
    if dbg:
        hT_dbg = nc.dram_tensor("hT_dbg", [128, 8 * T], BF16, kind="ExternalOutput").ap()
        hid_dbg = nc.dram_tensor("hid_dbg", [64, 8192], BF16, kind="ExternalOutput").ap()

    with ExitStack() as es:
        P = Prog(nc, es)

        def sb(name, shape, dt):
            return es.enter_context(nc.sbuf_tensor("sb_" + name, list(shape), dt))

        cbt = sb("cbt", [128, NCA], BF16)
        identf = sb("identf", [128, 128], F32)
        hT = sb("hT", [128, 8 * T], BF16)
        colsH = sb("colsH", [128, 120], F32)
        colsG = sb("colsG", [128, 56], F32)
        bps = sb("bps", [128, 4], F32)
        fcol = sb("fcol", [128, 8], F32)
        gmt = sb("gmt", [128, 1], F32)
        negt = sb("negt", [128, 64], F32)
        arena_t = sb("arena", [128, ARENA_WORDS], F32)
        pst = [es.enter_context(nc.psum_tensor("ps%d" % i, [128, 512], F32)) for i in range(8)]
        PS = PsPool(pst)
        A = Arena(arena_t, ARENA_WORDS)

        def cst(name, j=0, w=None):
            o, ww = _CB[name]
            if w is None:
                return cbt[:, o:o + ww]
            return cbt[:, o + j * w:o + (j + 1) * w]
        identb = cst("ident")
        onesb = cst("ones")
        hT3 = hT[:].rearrange("p (dc t) -> p dc t", dc=8)

        Bc = Buf("consts")
        Bparam = Buf("params")
        BhT = Buf("hT")

        P.dma("pool", DMA(cbt[:], I["cb16"][:, 0:NCA]), writes=[Bc])
        P.dma("sp", DMA(gmt[:], I["gm"][:, :]), "ldc", writes=[Bparam])
        P.dma("sp", DMA(negt[:], I["negt"][:, :]), "ldc", writes=[Bparam])
        for j, nm in enumerate(("b_f1", "freq_f1", "b_f2", "freq_f2")):
            P.dma("sp", DMA(fcol[0:64, j:j + 1], I[nm][:, :]), "ldc", writes=[Bparam])
            P.dma("sp", DMA(fcol[64:128, j:j + 1], I[nm][:, :]), "ldc", writes=[Bparam])
        P.op("act", ACT(identf[:], identb, AF.Copy), reads=[Bc], writes=[Bparam])

        hid2T = A.bf16(4096)
        wfo = A.bf16(4096)
        Bhid = Buf("hid2T")
        Bwfo = Buf("wfo")
        Bwfo2 = Buf("wfo_hi")
        P.dma("pool", DMA(wfo[0:64, :], I["w_f_out"][:, :]), writes=[Bwfo])
        P.dma("pool", DMA(wfo[64:128, :], I["w_f_out"][:, :]), writes=[Bwfo2])
        mark1 = A.off

        rowsH = A.f32(128, parts=120)
        rowsG = A.f32(128, parts=56)
        Brow = Buf("rows")
        binh = I["b_in"][0, 512:3584].rearrange("(r p) -> r p", p=128)
        P.dma("sp", DMA(rowsH[0:24, :], binh), "ldc", writes=[Brow])
        for k in range(3):
            P.dma("sp", DMA(rowsH[24 + 24 * k:48 + 24 * k, :], I["conv_w"][k, :].rearrange("(r p) -> r p", p=128)),
                  "ldc", writes=[Brow])
        P.dma("sp", DMA(rowsH[96:120, :], I["conv_b"][0, :].rearrange("(r p) -> r p", p=128)), "ldc", writes=[Brow])
        P.dma("sp", DMA(rowsG[0:16, :], I["b_in"][0, 3584:5632].rearrange("(r p) -> r p", p=128)), "ldc", writes=[Brow])
        P.dma("sp", DMA(rowsG[16:20, :], I["b_pool"][0, :].rearrange("(r p) -> r p", p=128)), "ldc", writes=[Brow])
        P.dma("sp", DMA(rowsG[20:24, :], I["pool_scale"][0, :].rearrange("(r p) -> r p", p=128)), "ldc", writes=[Brow])
        P.dma("sp", DMA(rowsG[24:56, :], I["b_ff1"][0, :].rearrange("(r p) -> r p", p=128)), "ldc", writes=[Brow])
        b = PS.get()
        P.op("pe", MM(pst[b][:, 0:120], rowsH[0:120, :], identf[0:120, 0:120]), reads=[Brow, Bparam], writes=[PS.b[b]])
        P.op("pe", MM(pst[b][:, 128:184], rowsG[0:56, :], identf[0:56, 0:56]), reads=[Brow, Bparam], writes=[PS.b[b]])
        P.op("dve", CP(colsH[:], pst[b][:, 0:120]), reads=[PS.b[b]], writes=[Bparam])
        P.op("dve", CP(colsG[:], pst[b][:, 128:184]), reads=[PS.b[b]], writes=[Bparam])
        PS.rel(b)
        P.op("dve", TT(bps[:], colsG[:, 16:20], colsG[:, 20:24], ALU.mult), reads=[Bparam], writes=[Bparam])
        fc2 = sb("fc2", [128, 4], F32)
        for j in range(2):
            P.op("dve", TS(fc2[:, 2 * j:2 * j + 1], fcol[:, 2 * j + 1:2 * j + 2], 1.0 / TWO_PI, ALU.mult),
                 reads=[Bparam], writes=[Bparam])
            P.op("dve", TT(fc2[:, 2 * j + 1:2 * j + 2], fc2[:, 2 * j:2 * j + 1], fcol[:, 2 * j:2 * j + 1], ALU.mult),
                 reads=[Bparam], writes=[Bparam])

        zTt = A.f32(8192, parts=33)
        w1t = A.f32(128, parts=33)
        w2t = A.f32(128, parts=64)
        vmb = A.bf16(8192)
        h1c = [A.f32(512) for _ in range(2)]
        tq = [A.f32(512) for _ in range(2)]
        kq = [A.f32(512) for _ in range(2)]
        Bz = Buf("zT"); Bh1c = [Buf("h1c0"), Buf("h1c1")]
        Btq = [Buf("tq0"), Buf("tq1")]; Bkq = [Buf("kq0"), Buf("kq1")]
        P.dma("sp", DMA(zTt, I["zT"][:, :]), "ldz", writes=[Bz])
        for hh in range(2):
            P.dma("sp", DMA(w1t[:, hh * 64:(hh + 1) * 64], I["w_f1"][:, :]), "ldz", writes=[Bz])
            P.dma("sp", DMA(w2t[:, hh * 64:(hh + 1) * 64], I["w_f2"][:, :]), "ldz", writes=[Bz])
        P.dma("sp", DMA(vmb, I["vmask"][0, :].partition_broadcast(128)), "ldz", writes=[Bz])
        for ch in range(16):
            cs = slice(ch * 512, (ch + 1) * 512)
            k = ch % 2
            for layer in range(2):
                b = PS.get()
                if layer == 0:
                    P.op("pe", MM(pst[b][:, :], w1t[0:33, :], zTt[0:33, cs]), reads=[Bz], writes=[PS.b[b]])
                else:
                    P.op("pe", MM(pst[b][:, :], w2t[0:64, :], h1c[k][0:64, :]), reads=[Bz, Bh1c[k]], writes=[PS.b[b]])
                P.op("act", ACT(tq[k], pst[b][:, :], AF.Identity, bias=fc2[:, 2 * layer + 1:2 * layer + 2],
                                scale=fc2[:, 2 * layer:2 * layer + 1]), reads=[PS.b[b], Bparam], writes=[Btq[k]])
                PS.rel(b)
                P.op("dve", TS(kq[k], tq[k], MAGIC, ALU.add), reads=[Btq[k]], writes=[Bkq[k]])
                P.op("dve", TS(kq[k], kq[k], -MAGIC, ALU.add), reads=[Bkq[k]], writes=[Bkq[k]])
                P.op("dve", TT(kq[k], tq[k], kq[k], ALU.subtract), reads=[Btq[k], Bkq[k]], writes=[Bkq[k]])
                if layer == 0:
                    P.op("act", ACT(h1c[k], kq[k], AF.Sin, scale=TWO_PI), reads=[Bkq[k]], writes=[Bh1c[k]])
                else:
                    P.op("act", ACT(tq[k], kq[k], AF.Sin, scale=TWO_PI), reads=[Bkq[k]], writes=[Btq[k]])
                    if ch < 8:
                        P.op("dve", TT(hid2T[0:64, cs], tq[k][0:64, :], vmb[0:64, cs], ALU.mult), reads=[Btq[k], Bz], writes=[Bhid])
                    else:
                        cs2 = slice((ch - 8) * 512, (ch - 7) * 512)
                        P.op("dve", TT(hid2T[64:128, cs2], tq[k][64:128, :], vmb[64:128, cs], ALU.mult),
                             reads=[Btq[k], Bz], writes=[Bhid])

        lng = A.f32(1024)
        lnb = A.f32(1024)
        Bln = Buf("lnrows")
        P.dma("sp", DMA(lng, I["ln_in_g"][0, :].partition_broadcast(128)), "ldz", writes=[Bln])
        P.dma("sp", DMA(lnb, I["ln_in_b"][0, :].partition_broadcast(128)), "ldz", writes=[Bln])
        xb = [A.f32(1024) for _ in range(3)]
        Bxb = [Buf("xb%d" % i) for i in range(3)]
        nb_ = [A.f32(1024) for _ in range(2)]
        Bnb = [Buf("nb%d" % i) for i in range(2)]
        hb_ = [A.bf16(1024) for _ in range(2)]
        Bhb = [Buf("hb%d" % i) for i in range(2)]
        stt = [A.f32(16) for _ in range(3)]
        Bst = [Buf("st%d" % i) for i in range(3)]

        def layer_norm(xin, Bxin, st, Bs, out_n, Bout):
            P.op("dve", lambda e: e.bn_stats(out=st[:, 0:6], in_=xin[:, 0:512]), reads=[Bxin], writes=[Bs])
            P.op("dve", lambda e: e.bn_stats(out=st[:, 6:12], in_=xin[:, 512:1024]), reads=[Bxin], writes=[Bs])
            P.op("dve", lambda e: e.bn_aggr(out=st[:, 12:14], in_=st[:, 0:12]), reads=[Bs], writes=[Bs])
            P.op("act", ACT(st[:, 14:15], st[:, 13:14], AF.Sqrt, bias=epsc[:, 0:1]), reads=[Bs, Bparam], writes=[Bs])
            P.op("dve", lambda e: e.reciprocal(out=st[:, 14:15], in_=st[:, 14:15]), reads=[Bs], writes=[Bs])
            P.op("dve", STT(st[:, 15:16], st[:, 12:13], -1.0, st[:, 14:15], ALU.mult, ALU.mult), reads=[Bs], writes=[Bs])
            P.op("act", ACT(out_n, xin, AF.Identity, bias=st[:, 15:16], scale=st[:, 14:15]), reads=[Bxin, Bs], writes=[Bout])

        epsc = sb("epsc", [128, 1], F32)
        P.op("dve", MSET(epsc[:], LN_EPS), writes=[Bparam])

        for i in range(NSUB):
            k3 = i % 3; k2 = i % 2
            P.dma("sp", DMA(xb[k3], I["x"][i * 128:(i + 1) * 128, :]), "ldx%d" % k3, writes=[Bxb[k3]])
            layer_norm(xb[k3], Bxb[k3], stt[k3], Bst[k3], nb_[k2], Bnb[k2])
            P.op("pool", TT(nb_[k2], nb_[k2], lng, ALU.mult), reads=[Bnb[k2], Bln], writes=[Bnb[k2]])
            P.op("dve", TT(hb_[k2], nb_[k2], lnb, ALU.add), reads=[Bnb[k2], Bln], writes=[Bhb[k2]])
            b = PS.get()
            psb = pst[b][:].bitcast(BF16)
            for dc in range(8):
                P.op("pe", TR(psb[:, dc * 128:(dc + 1) * 128], hb_[k2][:, dc * 128:(dc + 1) * 128], identb),
                     reads=[Bhb[k2], Bc], writes=[PS.b[b]])
            src = psb[:, 0:1024].rearrange("p (dc t) -> p dc t", dc=8)
            dst = hT3[:, :, i * 128:(i + 1) * 128]
            P.op("act" if i % 2 == 0 else "dve", CP(dst, src) if i % 2 else ACT(dst, src, AF.Copy),
                 reads=[PS.b[b]], writes=[BhT])
            PS.rel(b)
        if dbg:
            P.dma("sp", DMA(hT_dbg[:, :], hT[:]), "dbg", reads=[BhT])
        _barrier(P)
        A.off = mark1
        st_ = dict(nc=nc, P=P, I=I, A=A, PS=PS, pst=pst, cst=cst, identb=identb, onesb=onesb, hT3=hT3, hT=hT,
                   colsH=colsH, colsG=colsG, bps=bps, gmt=gmt, negt=negt,
                   hid2T=hid2T, wfo=wfo, Bc=Bc, Bparam=Bparam, BhT=BhT, Bhid=Bhid, Bwfo=Bwfo, Bwfo2=Bwfo2,
                   z2T_d=z2T_d, scr=scr, y_out=y_out, layer_norm=layer_norm, identf=identf)
        Bz2 = Buf("z2T_d")
        st_["Bz2"] = Bz2
        dumped = set()

        def dump(name, ap, bufs, dt=BF16):
            if not dbg or name in dumped:
                return
            dumped.add(name)
            d = nc.dram_tensor(name, list(ap.shape), dt, kind="ExternalOutput").ap()
            P.dma("sp", DMA(d, ap), "dbg", reads=bufs)
        st_["dump"] = dump
        _phase1(st_, ngroups)
        fin = [Bz2]
        if phase2:
            _barrier(P)
            A.off = 0
            fin = _phase2(st_, ntiles)
        P.final_wait("sp", fin)
        P.emit()
    return nc


def _cast_jobs(st_):
    I, scr = st_["I"], st_["scr"]
    jobs = []

    def tiled(dst, src, kc, c0, nch):
        rs = slice(kc * 128, (kc + 1) * 128)
        jobs.append((dst[:, :, kc, :], src[rs, c0:c0 + nch * 128].rearrange("p (ch j) -> ch p j", j=128)))
    for r in range(8):
        tiled(scr["wgt_s"], I["w_in"], r, 3584, 16)
        tiled(scr["whp_s"], I["w_hyena_proj"], r, 0, 8)
        tiled(scr["wf1_s"], I["w_ff1"], r, 0, 32)
    for r in range(4):
        tiled(scr["wpp_s"], I["w_pool_proj"], r, 0, 8)
    jobs.append((scr["wa_s"][:, :], I["w_in"][:, 0:512]))
    jobs.append((scr["wpool_s"][:, :], I["w_pool"][:, :]))
    for r in range(2):
        rs = slice(r * 512, (r + 1) * 512)
        jobs.append((scr["wo_s"][rs, :], I["w_o"][rs, :]))
    for r in range(4):
        rs = slice(r * 1024, (r + 1) * 1024)
        jobs.append((scr["wf2_s"][rs, :], I["w_ff2"][rs, :]))
    return jobs


def _phase1(st_, ngroups):
    P, I, A, PS, pst, cst = st_["P"], st_["I"], st_["A"], st_["PS"], st_["pst"], st_["cst"]
    hT3, colsH, gmt, negt = st_["hT3"], st_["colsH"], st_["gmt"], st_["negt"]
    hid2T, wfo, identb, onesb, identf = st_["hid2T"], st_["wfo"], st_["identb"], st_["onesb"], st_["identf"]
    Bc, Bparam, BhT, Bhid, Bwfo, Bz2 = st_["Bc"], st_["Bparam"], st_["BhT"], st_["Bhid"], st_["Bwfo"], st_["Bz2"]
    Bwfo2 = st_["Bwfo2"]
    z2T_d = st_["z2T_d"]
    dump = st_["dump"]
    Bscr = Buf("scratchw")
    st_["Bscr"] = Bscr
    jobs = _cast_jobs(st_)
    jpos = 0

    wg = A.bf16(3 * 8 * 128); Bwg = [Buf("wg%d" % i) for i in range(3)]
    wg4 = wg.rearrange("p (t dc j) -> p t dc j", t=3, dc=8)
    absdec_o = [A.f32(256), A.f32(256)]; Bdec_o = [Buf("absdec0"), Buf("absdec1")]
    hbg_o = [A.f32(128, parts=1), A.f32(128, parts=1)]; Bhbg_o = [Buf("hbg0"), Buf("hbg1")]
    us = [A.f32(2052), None]
    t1 = A.f32(1024); Bt1 = Buf("t1")
    t1h = [t1[:, 0:512], t1[:, 512:1024]]; Bt1h = [Buf("t1h0"), Buf("t1h1")]
    for bb in Bt1h:
        bb.al.append(Bt1); Bt1.al.append(bb)
    ucT = A.bf16(4096); BucT = Buf("ucT")
    v_tm = A.bf16(4096); Bv = Buf("v_tm")
    x1_tm = A.bf16(4096); Bx1 = Buf("x1_tm")
    qbuf = [A.bf16(8193), A.bf16(8194)]; Bqs = [Buf("qA"), Buf("qB")]
    Wt = [A.f32(512) for _ in range(2)]; BWt = [[Buf("Wt%d_%d" % (i, k)) for k in range(4)] for i in range(2)]
    qtmp = [A.f32(512) for _ in range(2)]; Bqt = [Buf("qt0"), Buf("qt1")]
    nrmcol = A.f32(2); wsc = A.f32(4); Bwsc = Buf("wsc")
    hrow = A.f32(128, parts=1)
    qabs = [A.bf16(512) for _ in range(2)]; Bqa = [Buf("qa0"), Buf("qa1")]
    nfull = A.f32(512); nrm = A.f32(128); Bnrm = Buf("nrm")
    KABs = [A.bf16(4096), ucT]; BKs = [Buf("KAB0"), Buf("KAB1")]
    BKs[1].al.append(BucT); BucT.al.append(BKs[1])
    Asb = [A.bf16(512) for _ in range(4)]; BAsb = [Buf("Asb%d" % i) for i in range(4)]
    t1b = t1.bitcast(BF16)
    P1s = [A.bf16(1024), t1b[:, 0:1024]]; P2s = [A.bf16(1024), t1b[:, 1024:2048]]
    BP1s = [[Buf("P1_%d_%d" % (i, k)) for k in range(2)] for i in range(2)]
    BP2s = [[Buf("P2_%d_%d" % (i, k)) for k in range(2)] for i in range(2)]
    for bb in BP1s[1]:
        bb.al.append(Bt1h[0]); Bt1h[0].al.append(bb)
    for bb in BP2s[1]:
        bb.al.append(Bt1h[1]); Bt1h[1].al.append(bb)
    Zt = [A.bf16(1024) for _ in range(2)]; BZt = [Buf("Zt0"), Buf("Zt1")]
    z2o = [qbuf[1][:, 6146:7170], qbuf[1][:, 7170:8194]]; Bz2o = [Buf("z2o0"), Buf("z2o1")]
    for bb in Bz2o:
        bb.al.append(Bqs[1]); Bqs[1].al.append(bb)

    us[1] = A.f32(2052)
    Bub = [[Buf("u%d_%d" % (s_, k)) for k in range(4)] for s_ in range(2)]
    Buh = [[Buf("u%d_hl" % s_), Buf("u%d_hr" % s_)] for s_ in range(2)]
    Fb = cst("Fb"); Fneg = cst("Fneg"); T1c = cst("T1"); T2c = cst("T2"); Cb = cst("Cb"); Sbn = cst("Sbn")
    SelR = [cst("SelR", j, 128) for j in range(4)]
    SelI = [cst("SelI", j, 128) for j in range(4)]
    FK = [[cst("FK", k * 2 + j, 128) for j in range(2)] for k in range(4)]

    for o_ in range(2):
        P.op("pool", MSET(qbuf[o_][:, 0:1], 0.0), writes=[Bqs[o_]])
    qcs = [qb[:, 1:8193].rearrange("p (c r) -> p c r", r=64) for qb in qbuf]
    qvs = [qb[:, 1:8193].rearrange("p (c r) -> p r c", r=64) for qb in qbuf]
    vtm3 = v_tm.rearrange("p (c a) -> p a c", a=32)
    x1tm3 = x1_tm.rearrange("p (c a) -> p a c", a=32)
    asb_i = [0, 0]

    def inproj_conv(G, tt, ncol=None):
        bcol = colsH[:, tt * 8 + G:tt * 8 + G + 1]
        w0 = colsH[:, 24 + tt * 8 + G:24 + tt * 8 + G + 1]
        w1 = colsH[:, 48 + tt * 8 + G:48 + tt * 8 + G + 1]
        w2 = colsH[:, 72 + tt * 8 + G:72 + tt * 8 + G + 1]
        cbc = colsH[:, 96 + tt * 8 + G:96 + tt * 8 + G + 1]
        if ncol is not None:
            for k, src in enumerate((w0, w1, w2, cbc)):
                P.op("dve", TT(wsc[:, k:k + 1], src, ncol, ALU.mult), reads=[Bparam, Bnrm] + Bt1h + [BucT], writes=[Bwsc])
            w0, w1, w2, cbc = (wsc[:, k:k + 1] for k in range(4))
        cnt = [0]

        def conv_block(s, k):
            u = us[s]
            base = 1 + k * 512
            th_ = cnt[0] % 2
            cnt[0] += 1
            rds = [Bub[s][k], Bub[s][k - 1] if k > 0 else Buh[s][0], Bub[s][k + 1] if k < 3 else Buh[s][1], Bparam, Bwsc]
            P.op("act", ACT(t1h[th_], u[:, base:base + 512], AF.Identity, bias=cbc, scale=w1), reads=rds, writes=[Bt1h[th_]])
            P.op("dve", STT(t1h[th_], u[:, base - 1:base + 511], w0, t1h[th_], ALU.mult, ALU.add),
                 reads=rds + [Bt1h[th_]], writes=[Bt1h[th_]])
            o0 = s * 2048 + k * 512
            P.op("dve", STT(ucT[:, o0:o0 + 512], u[:, base + 1:base + 513], w2, t1h[th_], ALU.mult, ALU.add),
                 reads=rds + [Bt1h[th_]], writes=[BucT])

        for s in range(2):
            u = us[s]
            th = 2048 if s == 0 else 2047
            hc = 2049 if s == 0 else 0
            zc = 0 if s == 0 else 2049
            Bhc = Buh[s][1] if s == 0 else Buh[s][0]
            Bzc = Buh[s][0] if s == 0 else Buh[s][1]
            b = PS.get()
            for dc in range(8):
                P.op("pe", MM(pst[b][:, 0:1], wg4[:, tt, dc, :], hT3[:, dc, th:th + 1], dc == 0, dc == 7),
                     reads=[Bwg[tt], BhT], writes=[PS.b[b]])
            P.op("act", ACT(u[:, hc:hc + 1], pst[b][:, 0:1], AF.Identity, bias=bcol),
                 reads=[PS.b[b], Bparam], writes=[Bhc])
            PS.rel(b)
            P.op("dve", TS(u[:, hc:hc + 1], u[:, hc:hc + 1], gmt[:, 0:1], ALU.mult), reads=[Bhc, Bparam], writes=[Bhc])
            P.op("dve", MSET(u[:, zc:zc + 1], 0.0), writes=[Bzc])
            for k in range(4):
                b = PS.get()
                t0 = s * 2048 + k * 512
                for dc in range(8):
                    P.op("pe", MM(pst[b][:, 0:512], wg4[:, tt, dc, :], hT3[:, dc, t0:t0 + 512], dc == 0, dc == 7),
                         reads=[Bwg[tt], BhT], writes=[PS.b[b]])
                P.op("act", ACT(u[:, 1 + k * 512:1 + (k + 1) * 512], pst[b][:, 0:512], AF.Identity, bias=bcol),
                     reads=[PS.b[b], Bparam], writes=[Bub[s][k]])
                PS.rel(b)
                if k > 0:
                    conv_block(s, k - 1)
            conv_block(s, 3)

    def to_time_major(dst3, Bdst):
        for a0 in range(0, 32, 8):
            b = PS.get()
            psb = pst[b][:].bitcast(BF16)
            for j in range(8):
                a = a0 + j
                P.op("pe", TR(psb[:, j * 128:(j + 1) * 128], ucT[:, a * 128:(a + 1) * 128], identb),
                     reads=[BucT, Bc], writes=[PS.b[b]])
            src = psb[:, 0:1024].rearrange("p (a c) -> p a c", a=8)
            eng = "act" if (a0 // 8) % 2 == 0 else "dve"
            P.op(eng, ACT(dst3[:, a0:a0 + 8, :], src, AF.Copy) if eng == "act" else CP(dst3[:, a0:a0 + 8, :], src),
                 reads=[PS.b[b]], writes=[Bdst])
            PS.rel(b)

    def filter_time(G, o):
        q, Bq, qc, qv = qbuf[o], Bqs[o], qcs[o], qvs[o]
        absdec, Bdec, hbg, Bhbg = absdec_o[o], Bdec_o[o], hbg_o[o], Bhbg_o[o]
        for k in range(2):
            P.dma("sp", DMA(absdec[:, k * 128:(k + 1) * 128],
                            I["decay_rate"][2 * o + k, G * 128:(G + 1) * 128].partition_broadcast(128)), writes=[Bdec])
        P.op("act", ACT(absdec, absdec, AF.Abs), reads=[Bdec], writes=[Bdec])
        P.dma("sp", DMA(hbg, I["hyena_bias"][0:1, o * HW + G * 128:o * HW + (G + 1) * 128]), writes=[Bhbg])
        pn = PS.get()
        for rb in range(16):
            dirn = 1 if rb < 8 else 0
            setc = (o * 2 + dirn)
            k2 = rb % 2
            b = PS.get()
            for k in range(4):
                r = rb * 4 + k
                if r < 32:
                    lhs, rhs, Bw_ = hid2T[0:64, r * 128:(r + 1) * 128], wfo[0:64, setc * 1024 + G * 128:setc * 1024 + (G + 1) * 128], Bwfo
                else:
                    lhs, rhs, Bw_ = (hid2T[64:128, (r - 32) * 128:(r - 31) * 128],
                                     wfo[64:128, setc * 1024 + G * 128:setc * 1024 + (G + 1) * 128], Bwfo2)
                P.op("pe", MM(pst[b][:, k * 128:(k + 1) * 128], lhs, rhs), reads=[Bhid, Bw_], writes=[PS.b[b]])
                P.op("act", ACT(Wt[k2][:, k * 128:(k + 1) * 128], absdec[:, dirn * 128:(dirn + 1) * 128], AF.Exp,
                                scale=negt[:, r:r + 1]), reads=[Bdec, Bparam], writes=[BWt[k2][k]])
            P.op("dve", TT(qtmp[k2], pst[b][:, 0:512], Wt[k2], ALU.mult), reads=[PS.b[b]] + BWt[k2], writes=[Bqt[k2]])
            PS.rel(b)
            P.op("dve", STT(qabs[k2], qtmp[k2], -1.0, qtmp[k2], ALU.mult, ALU.max), reads=[Bqt[k2]], writes=[Bqa[k2]])
            if rb > 0:
                P.op("pe", MM(pst[pn][:, 0:512], onesb, qabs[1 - k2], rb == 1, False), reads=[Bc, Bqa[1 - k2]], writes=[PS.b[pn]])
            P.op("pool", CP(qv[:, rb * 4:rb * 4 + 4, :], qtmp[k2].rearrange("p (k c) -> p k c", k=4)),
                 reads=[Bqt[k2]], writes=[Bq])
            yield
        yield
        P.op("pe", MM(pst[pn][:, 0:512], onesb, qabs[1], False, True), reads=[Bc, Bqa[1]], writes=[PS.b[pn]])
        P.op("act", ACT(nfull, pst[pn][:, 0:512], AF.Copy), reads=[PS.b[pn]], writes=[Bnrm])
        PS.rel(pn)
        P.op("dve", TT(nrm, nfull[:, 0:128], nfull[:, 128:256], ALU.add), reads=[Bnrm], writes=[Bnrm])
        P.op("dve", TT(nrm, nrm, nfull[:, 256:384], ALU.add), reads=[Bnrm], writes=[Bnrm])
        P.op("dve", TT(nrm, nrm, nfull[:, 384:512], ALU.add), reads=[Bnrm], writes=[Bnrm])
        P.op("dve", TS(nrm, nrm, 1e-6, ALU.add), reads=[Bnrm], writes=[Bnrm])
        P.op("dve", TT(hrow, nrm[0:1, :], hbg[0:1, :], ALU.mult), reads=[Bnrm, Bhbg], writes=[Bnrm])
        P.op("dve", TT(qc[0:1, :, 32], qc[0:1, :, 32], hrow, ALU.add), reads=[Bq, Bnrm], writes=[Bq])
        P.op("dve", lambda e: e.reciprocal(out=nrm, in_=nrm), reads=[Bnrm], writes=[Bnrm])
        yield
        yield
        b = PS.get()
        P.op("pe", MM(pst[b][:, 0:1], nrm[0:1, :], identf[0:1, 0:1]), reads=[Bnrm, Bparam], writes=[PS.b[b]])
        P.op("dve", CP(nrmcol[:, o:o + 1], pst[b][:, 0:1]), reads=[PS.b[b]], writes=[Bnrm])
        PS.rel(b)
        yield

    def filter_spectra(sg, kb, o):
        KAB, BK = KABs[kb], BKs[kb]
        q, Bq = qbuf[o], Bqs[o]

        def f2(qp):
            pa = PS.get()
            for pp in range(2):
                c0 = sg * 16 + qp * 4 + pp * 2
                half = pst[pa][:, pp * 256:pp * 256 + 256]
                P.op("pe", MM(half, q[:, 1 + c0 * 64:1 + c0 * 64 + 128], Fb, True, False), reads=[Bq, Bc], writes=[PS.b[pa]])
                P.op("pe", MM(half, q[:, c0 * 64:c0 * 64 + 128], Fneg, False, True), reads=[Bq, Bc], writes=[PS.b[pa]])
            ai = asb_i[0] % 2
            asb_i[0] += 1
            P.op("act", ACT(Asb[ai], pst[pa][:, 0:512], AF.Copy), reads=[PS.b[pa]], writes=[BAsb[ai]])
            PS.rel(pa)
            return ai

        def f3(qp, ai):
            pk = [PS.get(), PS.get()]
            for j in range(2):
                for kind in range(2):
                    for ri in range(2):
                        for pp in range(2):
                            out = pst[pk[pp]][:, (j * 2 + kind) * 128:(j * 2 + kind + 1) * 128]
                            P.op("pe", MM(out, FK[kind * 2 + ri][j], Asb[ai][:, pp * 256 + ri * 128:pp * 256 + ri * 128 + 128],
                                          ri == 0, ri == 1), reads=[Bc, BAsb[ai]], writes=[PS.b[pk[pp]]])
            for pp in range(2):
                pidx = qp * 2 + pp
                eng = "act" if pp == 0 else "dve"
                dst = KAB[:, pidx * 512:(pidx + 1) * 512]
                P.op(eng, ACT(dst, pst[pk[pp]][:, 0:512], AF.Copy) if eng == "act" else CP(dst, pst[pk[pp]][:, 0:512]),
                     reads=[PS.b[pk[pp]]], writes=[BK])
                PS.rel(pk[pp])

        ai_prev = f2(0)
        yield
        for qp in range(4):
            ai_next = f2(qp + 1) if qp + 1 < 4 else None
            if ai_next is not None:
                yield
            f3(qp, ai_prev)
            ai_prev = ai_next
            yield

    def conv_data(sg, o, kb):
        KAB, BK = KABs[kb], BKs[kb]
        KAB4 = KAB.rearrange("p (c k f) -> p c k f", k=2, f=128)
        src = v_tm if o == 0 else x1_tm
        Bsrc = Bv if o == 0 else Bx1
        zi = sg % 2

        def d1(oc):
            pa = PS.get()
            for qd in range(2):
                cq = sg * 16 + oc * 8 + qd * 4
                P.op("pe", MM(pst[pa][:, qd * 256:(qd + 1) * 256], src[:, cq * 32:cq * 32 + 128], Fb),
                     reads=[Bsrc, Bc], writes=[PS.b[pa]])
            ai = 2 + asb_i[1] % 2
            asb_i[1] += 1
            P.op("act", ACT(Asb[ai], pst[pa][:, 0:512], AF.Copy), reads=[PS.b[pa]], writes=[BAsb[ai]])
            PS.rel(pa)
            return ai

        def d2(oc, ai):
            pi = (sg * 2 + oc) % 2
            P1, P2, BP1, BP2 = P1s[pi], P2s[pi], BP1s[pi], BP2s[pi]
            pu = [PS.get(), PS.get()]
            for j in range(4):
                for part in range(2):
                    for qd in range(2):
                        P.op("pe", MM(pst[pu[qd]][:, j * 128:(j + 1) * 128], (SelR if part == 0 else SelI)[j],
                                      Asb[ai][:, qd * 256 + part * 128:qd * 256 + part * 128 + 128], part == 0, part == 1),
                             reads=[Bc, BAsb[ai]], writes=[PS.b[pu[qd]]])
            for qd in range(2):
                ch0 = oc * 8 + qd * 4
                uv = pst[pu[qd]][:, 0:512].rearrange("p (c f) -> p c f", c=4)
                P.op("dve", TT(P1[:, qd * 512:(qd + 1) * 512].rearrange("p (c f) -> p c f", c=4), uv,
                               KAB4[:, ch0:ch0 + 4, 0, :], ALU.mult), reads=[PS.b[pu[qd]], BK], writes=[BP1[qd]])
                P.op("dve", TT(P2[:, qd * 512:(qd + 1) * 512].rearrange("p (c f) -> p c f", c=4), uv,
                               KAB4[:, ch0:ch0 + 4, 1, :], ALU.mult), reads=[PS.b[pu[qd]], BK], writes=[BP2[qd]])
                PS.rel(pu[qd])

        def d4(oc):
            pi = (sg * 2 + oc) % 2
            P1, P2, BP1, BP2 = P1s[pi], P2s[pi], BP1s[pi], BP2s[pi]
            pz = PS.get()
            for ch in range(8):
                P.op("pe", MM(pst[pz][:, ch * 64:(ch + 1) * 64], P1[:, ch * 128:(ch + 1) * 128], T1c, True, False),
                     reads=[BP1[ch // 4], Bc], writes=[PS.b[pz]])
                P.op("pe", MM(pst[pz][:, ch * 64:(ch + 1) * 64], P2[:, ch * 128:(ch + 1) * 128], T2c, False, True),
                     reads=[BP2[ch // 4], Bc], writes=[PS.b[pz]])
            P.op("act", ACT(Zt[zi][:, oc * 512:(oc + 1) * 512], pst[pz][:, 0:512], AF.Copy),
                 reads=[PS.b[pz]], writes=[BZt[zi]])
            PS.rel(pz)

        a0 = d1(0)
        yield
        a1 = d1(1)
        yield
        d2(0, a0)
        yield
        d4(0)
        d2(1, a1)
        yield
        d4(1)
        yield
        Z4 = Zt[zi].rearrange("p (c ri a) -> p c ri a", ri=2, a=32)
        py = PS.get()
        P.op("pe", MM(pst[py][:, 0:512], Cb, Z4[:, :, 0, :], True, False), reads=[Bc, BZt[zi]], writes=[PS.b[py]])
        P.op("pe", MM(pst[py][:, 0:512], Sbn, Z4[:, :, 1, :], False, True), reads=[Bc, BZt[zi]], writes=[PS.b[py]])
        cs = slice(sg * 512, (sg + 1) * 512)
        if o == 0:
            P.op("dve", TT(x1_tm[:, cs], pst[py][:, 0:512], x1_tm[:, cs], ALU.mult), reads=[PS.b[py], Bx1], writes=[Bx1])
        else:
            P.op("act", ACT(v_tm[:, cs], pst[py][:, 0:512], AF.Copy), reads=[PS.b[py]], writes=[Bv])
        PS.rel(py)
        yield

    def conv_pass(o, extra=None):
        def chain(fn):
            for sg in range(8):
                yield from fn(sg)
        fgen = chain(lambda sg: filter_spectra(sg, sg % 2, o))
        dgen = chain(lambda sg: conv_data(sg, o, sg % 2))
        NF, ND, NX, DT = 8.0, 6.0, 20.0, 48.0
        fpos = dpos = xpos = 0
        fdone = ddone = False
        xdone = extra is None
        while not (fdone and ddone and xdone):
            if not fdone and (ddone or fpos / NF < dpos / ND + 1.0):
                try:
                    next(fgen); fpos += 1
                except StopIteration:
                    fdone = True
            elif not xdone and (ddone or xpos / NX <= dpos / DT):
                try:
                    next(extra); xpos += 1
                except StopIteration:
                    xdone = True
            elif not ddone:
                try:
                    next(dgen); dpos += 1
                except StopIteration:
                    ddone = True

    def drain(gen):
        for _ in gen:
            pass

    for G in range(ngroups):
        for tt, c0 in ((0, 512), (1, 1536), (2, 2560)):
            src = I["w_in"][:, c0 + G * 128:c0 + (G + 1) * 128].rearrange("(dc p) j -> p dc j", p=128)
            P.dma("pool", DMA(wg4[:, tt, :, :], src), writes=[Bwg[tt]])
        nj = (len(jobs) + ngroups - 1) // ngroups
        for dst, srcw in jobs[jpos:jpos + nj]:
            P.dma("pool", DMA(dst, srcw), writes=[Buf("scr%d" % jpos)])
            jpos += 1
        if G == 0:
            drain(filter_time(0, 0))
        inproj_conv(G, 0, nrmcol[:, 0:1])
        to_time_major(x1tm3, Bx1)
        inproj_conv(G, 2)
        to_time_major(vtm3, Bv)
        conv_pass(0, extra=filter_time(G, 1))
        conv_pass(1, extra=(filter_time(G + 1, 0) if G + 1 < ngroups else None))
        inproj_conv(G, 1, nrmcol[:, 1:2])
        for a0 in range(0, 32, 8):
            b = PS.get()
            psb = pst[b][:].bitcast(BF16)
            for j in range(8):
                P.op("pe", TR(psb[:, j * 128:(j + 1) * 128], vtm3[:, a0 + j, :], identb), reads=[Bv, Bc], writes=[PS.b[b]])
            k2 = (a0 // 8) % 2
            P.op("dve", TT(z2o[k2], psb[:, 0:1024], ucT[:, a0 * 128:a0 * 128 + 1024], ALU.mult),
                 reads=[PS.b[b], BucT], writes=[Bz2o[k2]])
            PS.rel(b)
            P.dma("sp", DMA(z2T_d[G * 128:(G + 1) * 128, a0 * 128:a0 * 128 + 1024], z2o[k2]), "stz",
                  reads=[Bz2o[k2]], writes=[Bz2])
    for dst, srcw in jobs[jpos:]:
        P.dma("pool", DMA(dst, srcw), writes=[Buf("scr%d" % jpos)])
        jpos += 1


def _phase2(st_, ntiles):
    nc, P, I, A, PS, pst, cst = st_["nc"], st_["P"], st_["I"], st_["A"], st_["PS"], st_["pst"], st_["cst"]
    hT3, colsG, bps, identb, identf = st_["hT3"], st_["colsG"], st_["bps"], st_["identb"], st_["identf"]
    Bc, Bparam, BhT = st_["Bc"], st_["Bparam"], st_["BhT"]
    z2T_d, scr, y_out, layer_norm, dump = st_["z2T_d"], st_["scr"], st_["y_out"], st_["layer_norm"], st_["dump"]
    NSLOT, LA = 12, 8

    cbP = A.bf16(36 * 128); BcbP = Buf("cbP")
    rows = [A.f32(1024) for _ in range(6)]
    Brows = Buf("lnrows2")
    ring = [A.bf16(1024) for _ in range(NSLOT)]; Bring = [Buf("ring%d" % i) for i in range(NSLOT)]
    wpool = A.bf16(512); Bwpool = Buf("wpool")
    lcols = A.f32(16); Blc = Buf("lcols")
    h_tok = [A.f32(1024) for _ in range(4)]; Bh = [Buf("h_tok%d" % i) for i in range(4)]
    stt = [A.f32(16) for _ in range(4)]; Bst = [Buf("st2_%d" % i) for i in range(4)]
    n1bf = [A.bf16(1024) for _ in range(2)]; Bn1 = [Buf("n1bf0"), Buf("n1bf1")]
    h1T = A.bf16(8 * 512); Bh1T = Buf("h1T")
    rtmp = [A.f32(512) for _ in range(2)]; Brt = [Buf("rtmp0"), Buf("rtmp1")]
    xoff = A.off
    a_sub = [A.bf16(512) for _ in range(6)]; Ba = [Buf("a_sub%d" % i) for i in range(6)]
    pT = [A.bf16(512) for _ in range(4)]; BpT = [Buf("pT%d" % i) for i in range(4)]
    qT = [A.bf16(512) for _ in range(4)]; BqT = [Buf("qT%d" % i) for i in range(4)]
    z2t = A.bf16(8 * 512); Bz2t = Buf("z2t")
    gT = [A.bf16(512) for _ in range(4)]; BgT = [Buf("gT%d" % i) for i in range(4)]
    mt = [A.f32(512) for _ in range(2)]; Bmt = [Buf("mt0"), Buf("mt1")]
    mT = A.bf16(8 * 512); BmT = Buf("mT")
    xend = A.off
    A.off = xoff
    uT = A.bf16(32 * 512)
    BuT = [Buf("uT%d" % i) for i in range(32)]
    A.off = max(A.off, xend)
    mix = list(zip([xoff] * 0, []))
    def rng(ap_words_start, nwords):
        return (ap_words_start, ap_words_start + nwords)
    pos = xoff
    mixbufs = []
    for bl, nw in ([(b, 256) for b in Ba] + [(b, 256) for b in BpT] + [(b, 256) for b in BqT] + [(Bz2t, 2048)]
                   + [(b, 256) for b in BgT] + [(b, 512) for b in Bmt] + [(BmT, 2048)]):
        mixbufs.append((bl, pos, pos + nw))
        pos += nw
    assert pos == xend, (pos, xend)
    for k in range(32):
        lo, hi = xoff + k * 256, xoff + (k + 1) * 256
        for bl, a0, a1 in mixbufs:
            if a0 < hi and lo < a1:
                BuT[k].al.append(bl)
                bl.al.append(BuT[k])
    uT3 = uT.rearrange("p (fc t) -> p fc t", fc=32)
    h1T3 = h1T.rearrange("p (dc t) -> p dc t", dc=8)
    mT3 = mT.rearrange("p (dc t) -> p dc t", dc=8)
    z2t3 = z2t.rearrange("p (cc t) -> p cc t", cc=8)

    o_pool = _CB["pool"][0]
    P.dma("pool", DMA(cbP, I["cb16"][:, o_pool:o_pool + 36 * 128]), writes=[BcbP])
    P.dma("sp", DMA(wpool.rearrange("p (g d) -> p g d", g=4), scr["wpool_s"].rearrange("(g c) d -> c g d", c=128)),
          writes=[Bwpool])
    srcs = ("ln_in_g", "ln_in_b", "ln1_g", "ln1_b", "ln2_g", "ln2_b")
    for k, nm in enumerate(srcs):
        P.dma("sp", DMA(rows[k], I[nm][0, :].partition_broadcast(128)), writes=[Brows])
    tmpr = h_tok[0]
    P.dma("sp", DMA(tmpr, I["b_o"][0, :].partition_broadcast(128)), writes=[Bh[0]])
    P.op("dve", STT(rows[1], rows[1], ALPHA, tmpr, ALU.mult, ALU.add), reads=[Brows, Bh[0]], writes=[Brows])
    P.op("dve", TS(rows[0], rows[0], ALPHA, ALU.mult), reads=[Brows], writes=[Brows])
    rws = h_tok[1]
    P.dma("sp", DMA(rws[0:8, 0:128], I["ln1_g"][0, :].rearrange("(r p) -> r p", p=128)), writes=[Bh[1]])
    P.dma("sp", DMA(rws[8:16, 0:128], I["ln1_b"][0, :].rearrange("(r p) -> r p", p=128)), writes=[Bh[1]])
    b = PS.get()
    P.op("pe", MM(pst[b][:, 0:16], rws[0:16, 0:128], identf[0:16, 0:16]), reads=[Bh[1], Bparam], writes=[PS.b[b]])
    P.op("dve", CP(lcols, pst[b][:, 0:16]), reads=[PS.b[b]], writes=[Blc])
    PS.rel(b)
    tmp2 = h_tok[2]
    P.dma("sp", DMA(tmp2, I["b_ff2"][0, :].partition_broadcast(128)), writes=[Bh[2]])
    P.op("dve", STT(rows[3], rows[3], ALPHA, tmp2, ALU.mult, ALU.add), reads=[Brows, Bh[2]], writes=[Brows])
    P.op("dve", TS(rows[2], rows[2], ALPHA, ALU.mult), reads=[Brows], writes=[Brows])

    wa_v = scr["wa_s"].rearrange("(dc p) n -> p dc n", p=128)
    wo_v = scr["wo_s"].rearrange("(dc p) n -> p dc n", p=128)
    wf2_v = scr["wf2_s"].rearrange("(fc p) n -> p fc n", p=128)
    tile_items = []
    for i2 in range(4):
        tile_items.append(((2, 512), wa_v[:, 2 * i2:2 * i2 + 2, :]))
    for dmc in range(8):
        tile_items.append(((4, 128), scr["wpp_s"][dmc]))
        tile_items.append(((8, 128), scr["whp_s"][dmc]))
        tile_items.append(((8, 128), scr["wgt_s"][dmc]))
        tile_items.append(((8, 128), scr["wgt_s"][8 + dmc]))
    for half in range(2):
        for i2 in range(4):
            tile_items.append(((2, 512), wo_v[:, 2 * i2:2 * i2 + 2, half * 512:(half + 1) * 512]))
    for fc in range(32):
        tile_items.append(((8, 128), scr["wf1_s"][fc]))
    for half in range(2):
        for i2 in range(16):
            tile_items.append(((2, 512), wf2_v[:, 2 * i2:2 * i2 + 2, half * 512:(half + 1) * 512]))
    items = tile_items * ntiles
    wst = {"issue": 0, "use": 0}

    def wview(k):
        (a, bb), _ = items[k]
        return ring[k % NSLOT][:, 0:a * bb].rearrange("p (a b) -> p a b", a=a)

    def wget():
        while wst["issue"] < len(items) and wst["issue"] <= wst["use"] + LA:
            k = wst["issue"]
            P.dma("sp", DMA(wview(k), items[k][1]), writes=[Bring[k % NSLOT]])
            wst["issue"] += 1
        k = wst["use"]
        wst["use"] += 1
        return wview(k), Bring[k % NSLOT]

    def pool_blocks(i):
        if i == 0:
            dv = 1
        elif i == 31:
            dv = 2
        elif i == 15:
            dv = 3
        elif i == 16:
            dv = 4
        else:
            dv = 0
        pv = None if i == 0 else (6 if i == 16 else 5)
        nv = None if i == 31 else (8 if i == 15 else 7)
        return pv, dv, nv

    def pblk(k, g):
        o = (k * 4 + g) * 128
        return cbP[:, o:o + 128]

    def st_A(j):
        tsl = slice(j * 512, (j + 1) * 512)
        for sub in range(4):
            i = 4 * j + sub
            P.dma("sp", DMA(h_tok[sub], I["x"][i * 128:(i + 1) * 128, :]), writes=[Bh[sub]])
            layer_norm(h_tok[sub], Bh[sub], stt[sub], Bst[sub], h_tok[sub], Bh[sub])
            P.op("dve", TT(h_tok[sub], h_tok[sub], rows[0], ALU.mult), reads=[Bh[sub], Brows], writes=[Bh[sub]])
            P.op("dve", TT(h_tok[sub], h_tok[sub], rows[1], ALU.add), reads=[Bh[sub], Brows], writes=[Bh[sub]])

    def st_BE(j):
        tsl = slice(j * 512, (j + 1) * 512)
        wa = [wget() for _ in range(4)]
        for sl in range(6):
            i = min(max(4 * j - 1 + sl, 0), 31)
            b = PS.get()
            for dc in range(8):
                wv, Bw = wa[dc // 2]
                P.op("pe", MM(pst[b][:, 0:512], hT3[:, dc, i * 128:(i + 1) * 128], wv[:, dc % 2, :], dc == 0, dc == 7),
                     reads=[BhT, Bw], writes=[PS.b[b]])
            P.op("act", ACT(a_sub[sl], pst[b][:, 0:512], AF.Copy), reads=[PS.b[b]], writes=[Ba[sl]])
            PS.rel(b)
            yield
        for g in range(4):
            b = PS.get()
            for sub in range(4):
                i = 4 * j + sub
                pv, dv, nv = pool_blocks(i)
                lst = []
                if pv is not None:
                    lst.append((sub, pv))
                lst.append((sub + 1, dv))
                if nv is not None:
                    lst.append((sub + 2, nv))
                for n_, (sl, kb) in enumerate(lst):
                    P.op("pe", MM(pst[b][:, sub * 128:(sub + 1) * 128], a_sub[sl][:, g * 128:(g + 1) * 128], pblk(kb, g),
                                  n_ == 0, n_ == len(lst) - 1), reads=[Ba[sl], BcbP], writes=[PS.b[b]])
            P.op("dve", CP(pT[g], pst[b][:, 0:512]), reads=[PS.b[b]], writes=[BpT[g]])
            PS.rel(b)
            b = PS.get()
            P.op("pe", MM(pst[b][:, 0:512], wpool[:, g * 128:(g + 1) * 128], pT[g]), reads=[Bwpool, BpT[g]], writes=[PS.b[b]])
            P.op("act", ACT(qT[g], pst[b][:, 0:512], AF.Identity, bias=bps[:, g:g + 1], scale=colsG[:, 20 + g:21 + g]),
                 reads=[PS.b[b], Bparam], writes=[BqT[g]])
            PS.rel(b)
            yield
        P.dma("sp", DMA(z2t3, z2T_d[:, tsl].rearrange("(cc p) t -> p cc t", p=128)), writes=[Bz2t])
        for dmc in range(8):
            wv, Bw = wget()
            b1 = PS.get()
            for gc in range(4):
                P.op("pe", MM(pst[b1][:, 0:512], wv[:, gc, :], qT[gc], gc == 0, gc == 3), reads=[Bw, BqT[gc]], writes=[PS.b[b1]])
            wv, Bw = wget()
            b2 = PS.get()
            for cc in range(8):
                P.op("pe", MM(pst[b2][:, 0:512], wv[:, cc, :], z2t3[:, cc, :], cc == 0, cc == 7), reads=[Bw, Bz2t], writes=[PS.b[b2]])
            gi = []
            for br in range(2):
                wv, Bw = wget()
                b3 = PS.get()
                for dc in range(8):
                    P.op("pe", MM(pst[b3][:, 0:512], wv[:, dc, :], hT3[:, dc, tsl], dc == 0, dc == 7),
                         reads=[Bw, BhT], writes=[PS.b[b3]])
                k = (dmc * 2 + br) % 4
                P.op("act", ACT(gT[k], pst[b3][:, 0:512], AF.Sigmoid, bias=colsG[:, br * 8 + dmc:br * 8 + dmc + 1]),
                     reads=[PS.b[b3], Bparam], writes=[BgT[k]])
                PS.rel(b3)
                gi.append(k)
            P.op("dve", TT(mt[0], pst[b1][:, 0:512], gT[gi[0]], ALU.mult), reads=[PS.b[b1], BgT[gi[0]]], writes=[Bmt[0]])
            PS.rel(b1)
            P.op("dve", TT(mt[1], pst[b2][:, 0:512], gT[gi[1]], ALU.mult), reads=[PS.b[b2], BgT[gi[1]]], writes=[Bmt[1]])
            PS.rel(b2)
            P.op("pool", TT(mT3[:, dmc, :], mt[0], mt[1], ALU.add), reads=[Bmt[0], Bmt[1]], writes=[BmT])
            yield

    def st_FI(j):
        tsl = slice(j * 512, (j + 1) * 512)
        for half in range(2):
            hs = slice(half * 512, (half + 1) * 512)
            bk = [PS.get() for _ in range(4)]
            for i2 in range(4):
                wv, Bw = wget()
                for dd in range(2):
                    dc = 2 * i2 + dd
                    for sub in range(4):
                        P.op("pe", MM(pst[bk[sub]][:, 0:512], mT3[:, dc, sub * 128:(sub + 1) * 128], wv[:, dd, :],
                                      dc == 0, dc == 7), reads=[BmT, Bw], writes=[PS.b[bk[sub]]])
            for sub in range(4):
                P.op("dve", TT(h_tok[sub][:, hs], pst[bk[sub]][:, 0:512], h_tok[sub][:, hs], ALU.add),
                     reads=[PS.b[bk[sub]], Bh[sub]], writes=[Bh[sub]])
                PS.rel(bk[sub])
        for sub in range(4):
            k2 = sub % 2
            layer_norm(h_tok[sub], Bh[sub], stt[sub], Bst[sub], h_tok[sub], Bh[sub])
            P.op("act", ACT(n1bf[k2], h_tok[sub], AF.Copy), reads=[Bh[sub]], writes=[Bn1[k2]])
            b = PS.get()
            psb = pst[b][:].bitcast(BF16)
            for dc in range(8):
                P.op("pe", TR(psb[:, dc * 128:(dc + 1) * 128], n1bf[k2][:, dc * 128:(dc + 1) * 128], identb),
                     reads=[Bn1[k2], Bc], writes=[PS.b[b]])
            for dc in range(8):
                P.op("act", ACT(h1T3[:, dc, sub * 128:(sub + 1) * 128], psb[:, dc * 128:(dc + 1) * 128], AF.Identity,
                                bias=lcols[:, 8 + dc:9 + dc], scale=lcols[:, dc:dc + 1]),
                     reads=[PS.b[b], Blc], writes=[Bh1T])
            PS.rel(b)
        for fc in range(32):
            wv, Bw = wget()
            b = PS.get()
            for dc in range(8):
                P.op("pe", MM(pst[b][:, 0:512], wv[:, dc, :], h1T3[:, dc, :], dc == 0, dc == 7), reads=[Bw, Bh1T], writes=[PS.b[b]])
            k2 = fc % 2
            P.op("dve", TS(rtmp[k2], pst[b][:, 0:512], colsG[:, 24 + fc:25 + fc], ALU.add, 0.0, ALU.max),
                 reads=[PS.b[b], Bparam], writes=[Brt[k2]])
            PS.rel(b)
            P.op("act", ACT(uT3[:, fc, :], rtmp[k2], AF.Square), reads=[Brt[k2]], writes=[BuT[fc]])
        for sub in range(4):
            P.op("dve", TT(h_tok[sub], h_tok[sub], rows[2], ALU.mult), reads=[Bh[sub], Brows], writes=[Bh[sub]])
            P.op("dve", TT(h_tok[sub], h_tok[sub], rows[3], ALU.add), reads=[Bh[sub], Brows], writes=[Bh[sub]])
        for half in range(2):
            hs = slice(half * 512, (half + 1) * 512)
            bk = [PS.get() for _ in range(4)]
            for i2 in range(16):
                wv, Bw = wget()
                for dd in range(2):
                    fc = 2 * i2 + dd
                    for sub in range(4):
                        P.op("pe", MM(pst[bk[sub]][:, 0:512], uT3[:, fc, sub * 128:(sub + 1) * 128], wv[:, dd, :],
                                      fc == 0, fc == 31), reads=[BuT[fc], Bw], writes=[PS.b[bk[sub]]])
            for sub in range(4):
                P.op("dve", TT(h_tok[sub][:, hs], pst[bk[sub]][:, 0:512], h_tok[sub][:, hs], ALU.add),
                     reads=[PS.b[bk[sub]], Bh[sub]], writes=[Bh[sub]])
                PS.rel(bk[sub])

    def st_J(j):
        tsl = slice(j * 512, (j + 1) * 512)
        for sub in range(4):
            i = 4 * j + sub
            layer_norm(h_tok[sub], Bh[sub], stt[sub], Bst[sub], h_tok[sub], Bh[sub])
            P.op("dve", TT(h_tok[sub], h_tok[sub], rows[4], ALU.mult), reads=[Bh[sub], Brows], writes=[Bh[sub]])
            P.op("dve", TT(h_tok[sub], h_tok[sub], rows[5], ALU.add), reads=[Bh[sub], Brows], writes=[Bh[sub]])
            P.dma("act", DMA(y_out[i * 128:(i + 1) * 128, :], h_tok[sub]), reads=[Bh[sub]])

    def st_JA(j):
        for sub in range(4):
            i = 4 * j + sub
            layer_norm(h_tok[sub], Bh[sub], stt[sub], Bst[sub], h_tok[sub], Bh[sub])
            P.op("dve", TT(h_tok[sub], h_tok[sub], rows[4], ALU.mult), reads=[Bh[sub], Brows], writes=[Bh[sub]])
            P.op("dve", TT(h_tok[sub], h_tok[sub], rows[5], ALU.add), reads=[Bh[sub], Brows], writes=[Bh[sub]])
            P.dma("act", DMA(y_out[i * 128:(i + 1) * 128, :], h_tok[sub]), reads=[Bh[sub]])
            yield
            if j + 1 < ntiles:
                i2 = 4 * (j + 1) + sub
                P.dma("sp", DMA(h_tok[sub], I["x"][i2 * 128:(i2 + 1) * 128, :]), writes=[Bh[sub]])
                layer_norm(h_tok[sub], Bh[sub], stt[sub], Bst[sub], h_tok[sub], Bh[sub])
                P.op("dve", TT(h_tok[sub], h_tok[sub], rows[0], ALU.mult), reads=[Bh[sub], Brows], writes=[Bh[sub]])
                P.op("dve", TT(h_tok[sub], h_tok[sub], rows[1], ALU.add), reads=[Bh[sub], Brows], writes=[Bh[sub]])
                yield

    def interleave(g1, n1, g2, n2):
        p1 = p2 = 0
        d1 = d2 = False
        while not (d1 and d2):
            if not d1 and (d2 or p1 * n2 <= p2 * n1):
                try:
                    next(g1); p1 += 1
                except StopIteration:
                    d1 = True
            else:
                try:
                    next(g2); p2 += 1
                except StopIteration:
                    d2 = True

    st_A(0)
    for _ in st_BE(0):
        pass
    for j in range(ntiles):
        st_FI(j)
        if j + 1 < ntiles:
            interleave(st_JA(j), 8, st_BE(j + 1), 18)
        else:
            for _ in st_JA(j):
                pass
    return Bh


_HC = {}


def _host_consts_cached(L, nseq):
    key = (L, nseq)
    if key not in _HC:
        _HC[key] = host_consts(L, nseq)
    return _HC[key]


def make_in_maps(inputs):
    f = lambda a: np.ascontiguousarray(np.asarray(a), dtype=np.float32)
    p = {k: np.asarray(v) for k, v in inputs.items()}
    shared = {
        "w_in": f(p["w_in"][0]), "b_in": f(p["b_in"][0]).reshape(1, -1),
        "w_pool": f(p["w_pool"][0]).reshape(512, 128), "b_pool": f(p["b_pool"][0]).reshape(1, 512),
        "pool_scale": f(p["pool_scale"][0]).reshape(1, 512), "w_pool_proj": f(p["w_pool_proj"][0]),
        "conv_w": f(p["conv_w"][0]), "conv_b": f(p["conv_b"][0]).reshape(1, -1),
        "w_f1": f(p["w_f1"][0]), "b_f1": f(p["b_f1"][0]).reshape(64, 1), "freq_f1": f(p["freq_f1"][0]).reshape(64, 1),
        "w_f2": f(p["w_f2"][0]), "b_f2": f(p["b_f2"][0]).reshape(64, 1), "freq_f2": f(p["freq_f2"][0]).reshape(64, 1),
        "w_f_out": f(p["w_f_out"][0]), "decay_rate": f(p["decay_rate"][0]),
        "hyena_bias": f(p["hyena_bias"][0]).reshape(1, -1),
        "w_hyena_proj": f(p["w_hyena_proj"][0]), "w_o": f(p["w_o"][0]), "b_o": f(p["b_o"][0]).reshape(1, -1),
        "ln1_g": f(p["ln1_g"][0]).reshape(1, -1), "ln1_b": f(p["ln1_b"][0]).reshape(1, -1),
        "w_ff1": f(p["w_ff1"][0]), "b_ff1": f(p["b_ff1"][0]).reshape(1, -1),
        "w_ff2": f(p["w_ff2"][0]), "b_ff2": f(p["b_ff2"][0]).reshape(1, -1),
        "ln2_g": f(p["ln2_g"][0]).reshape(1, -1), "ln2_b": f(p["ln2_b"][0]).reshape(1, -1),
        "ln_in_g": f(p["ln_in_g"]).reshape(1, -1), "ln_in_b": f(p["ln_in_b"]).reshape(1, -1),
    }
    xp = f(p["x_prompt"]); xs = f(p["x_sample"])
    maps = []
    for c in range(8):
        if c < 4:
            x = xp[c]
            hc = _host_consts_cached(4096, 1)
        else:
            x = xs[2 * (c - 4):2 * (c - 4) + 2].reshape(T, D)
            hc = _host_consts_cached(2048, 2)
        m = dict(shared)
        m["x"] = np.ascontiguousarray(x)
        m.update(hc)
        maps.append(m)
    return maps


_NC = {}


def kernel(**inputs):
    maps = make_in_maps(inputs)
    if "nc" not in _NC:
        _NC["nc"] = build_program()
    res = run_bass_kernel_spmd(_NC["nc"], maps, core_ids=list(range(8)))
    ys = [np.asarray(r["y"], dtype=np.float32) for r in res.results]
    y_prompt = np.stack(ys[0:4], axis=0)
    y_sample = np.concatenate([ys[c].reshape(2, 2048, D) for c in range(4, 8)], axis=0)
    return (y_prompt, y_sample)
```

```python
import math
from contextlib import ExitStack

import numpy as np
import ml_dtypes

import concourse.bass as bass
import concourse.mybir as mybir
from concourse.bass_utils import run_bass_kernel_spmd

F32 = mybir.dt.float32
BF16 = mybir.dt.bfloat16
AF = mybir.ActivationFunctionType
ALU = mybir.AluOpType
NPBF = ml_dtypes.bfloat16

D = 1024
T = 4096
NSUB = 32
HW = 1024
DFF = 4096
LN_EPS = 1e-5
ALPHA = 2.0 ** 0.25
MAGIC = float(1.5 * 2 ** 23)
TWO_PI = float(2 * np.pi)
POOL_WINDOWS = (2, 4, 8, 16)


class Buf:
    __slots__ = ("name", "w", "r", "excl", "const", "al")

    def __init__(self, name, excl=False, const=False):
        self.name = name
        self.w = None
        self.r = []
        self.excl = excl
        self.const = const
        self.al = []


class Prog:
    ENGS = ("pe", "act", "dve", "pool", "sp")

    def __init__(self, nc, es):
        self.nc = nc
        self.es = es
        self.items = {k: [] for k in self.ENGS}
        self.cnt = {k: 0 for k in self.ENGS}
        self.seen = {k: {} for k in self.ENGS}
        self.sems = {}
        for k in self.ENGS:
            self.sems[k] = es.enter_context(nc.semaphore("s_" + k))
        self.dma_cnt = {}

    def _waits(self, eng, reads, writes):
        need = {}

        def add(ev, rr=False):
            if ev is None:
                return
            k, v = ev
            if k == eng and (eng == "pe" or rr):
                return
            if need.get(k, 0) < v:
                need[k] = v
        for b in reads:
            add(b.w)
            if b.excl:
                for ev in b.r:
                    add(ev, rr=True)
        for b in writes:
            add(b.w)
            for ev in b.r:
                add(ev)
            for a in b.al:
                add(a.w)
                for ev in a.r:
                    add(ev)
        out = []
        seen = self.seen[eng]
        for k, v in need.items():
            if seen.get(k, 0) < v:
                seen[k] = v
                out.append((k, v))
        return out

    def _record(self, ev, reads, writes):
        for b in reads:
            if not b.const:
                b.r.append(ev)
        for b in writes:
            b.w = ev
            b.r = []

    def op(self, eng, fn, reads=(), writes=()):
        waits = self._waits(eng, reads, writes)
        self.cnt[eng] += 1
        ev = (eng, self.cnt[eng])
        self.items[eng].append((waits, fn, (eng, 1)))
        self._record(ev, reads, writes)
        return ev

    def dma(self, eng, fn, semkey=None, reads=(), writes=()):
        if eng == "pool":
            semkey = "dq_%d" % len(self.dma_cnt)
        elif writes:
            semkey = "d_" + writes[0].name
        else:
            semkey = "d_" + reads[0].name
        if semkey not in self.sems:
            self.sems[semkey] = self.es.enter_context(self.nc.semaphore("d%d" % len(self.sems)))
            self.dma_cnt[semkey] = 0
        waits = self._waits(eng, reads, writes)
        self.dma_cnt[semkey] += 16
        ev = (semkey, self.dma_cnt[semkey])
        self.items[eng].append((waits, fn, (semkey, 16)))
        self._record(ev, reads, writes)
        return ev

    def final_wait(self, eng, bufs):
        waits = self._waits(eng, bufs, bufs)
        self.items[eng].append((waits, None, None))

    def emit(self):
        nc = self.nc
        sems = self.sems
        items = self.items
        with nc.Block() as block:
            def run(e, lst):
                for waits, fn, inc in lst:
                    for k, v in waits:
                        e.wait_ge(sems[k], v)
                    if fn is not None:
                        fn(e).then_inc(sems[inc[0]], inc[1])

            @block.sync
            def _(e):
                run(e, items["sp"])

            @block.tensor
            def _(e):
                run(e, items["pe"])

            @block.scalar
            def _(e):
                run(e, items["act"])

            @block.vector
            def _(e):
                run(e, items["dve"])

            @block.gpsimd
            def _(e):
                run(e, items["pool"])


def conv_consts(L, nseq):
    Aq = L // 128
    G = 2 * Aq
    b = np.arange(128)[:, None]
    f = np.arange(128)[None, :]
    ph = np.pi * (2 * f + 1) * b / 256.0
    Fb = np.concatenate([np.cos(ph), -np.sin(ph)], axis=1)
    phn = np.pi * (2 * f + 1) * (b - 128) / 256.0
    Fneg = np.concatenate([np.cos(phn), -np.sin(phn)], axis=1)
    Fneg[0, :] = 0.0
    SelR = np.zeros((4, 128, 128)); SelI = np.zeros((4, 128, 128))
    g = np.arange(G)
    for j in range(4):
        for a in range(32):
            s, ap = divmod(a, Aq)
            th = 2 * np.pi * g * ap / G
            row = j * 32 + a
            SelR[j, row, s * G + g] = np.cos(th)
            SelR[j, row, 64 + s * G + g] = -np.sin(th)
            SelI[j, row, s * G + g] = np.sin(th)
            SelI[j, row, 64 + s * G + g] = np.cos(th)
    T1 = np.zeros((128, 64)); T2 = np.zeros((128, 64))
    for s in range(nseq):
        for gg in range(G):
            for ap in range(Aq):
                a = s * Aq + ap
                th = 2 * np.pi * gg * ap / G
                Gr = np.cos(th) / G; Gi = np.sin(th) / G
                top = s * G + gg; bot = 64 + s * G + gg
                T1[top, a] = Gr;   T1[top, 32 + a] = Gi
                T1[bot, a] = -Gi;  T1[bot, 32 + a] = Gr
                T2[top, a] = -Gi;  T2[top, 32 + a] = Gr
                T2[bot, a] = -Gr;  T2[bot, 32 + a] = -Gi
    ff = np.arange(128)[:, None]; bb = np.arange(128)[None, :]
    ph2 = np.pi * (2 * ff + 1) * bb / 256.0
    Cb = np.cos(ph2) / 128.0
    Sbn = -np.sin(ph2) / 128.0
    FK = np.zeros((4, 2, 128, 128))
    for j in range(2):
        for r in range(64):
            d = r - 32
            if abs(d) > Aq - 1:
                continue
            th = 2 * np.pi * g * d / G
            row = j * 64 + r
            for ro in range(2):
                for s in range(nseq):
                    cols = ro * 64 + s * G + g
                    FK[0, j, row, cols] = np.cos(th)
                    FK[1, j, row, cols] = np.sin(th)
                    FK[2, j, row, cols] = -np.sin(th)
                    FK[3, j, row, cols] = np.cos(th)
    return dict(Fb=Fb, Fneg=Fneg, SelR=SelR, SelI=SelI, T1=T1, T2=T2, Cb=Cb, Sbn=Sbn, FK=FK)


def lag_table(L):
    p = np.arange(128)[None, :]
    r = np.arange(64)[:, None]
    lag = np.where(r < 32, 128 * (32 - r) - p, 128 * (r - 32) + p)
    valid = np.where(r < 32, (lag >= 1) & (lag <= L - 1), lag <= L - 1)
    return lag, valid


def pool_mats(L, nseq):
    def dense(g, n):
        w = POOL_WINDOWS[g]
        M = np.zeros((n, n))
        for t in range(n):
            lo = max(t - w // 2, 0); hi = min(t + (w - w // 2), n)
            M[t, lo:hi] = 1.0 / (hi - lo)
            M[t, t] -= 1.0
        return M.T
    tab = np.zeros((9, 4, 128, 128))
    for g in range(4):
        Mt = dense(g, 512)
        Dmid = Mt[128:256, 128:256]; Dfirst = Mt[0:128, 0:128]; Dlast = Mt[384:512, 384:512]
        Pmid = Mt[0:128, 128:256]
        Nmid = Mt[256:384, 128:256]
        tab[0, g] = Dmid; tab[1, g] = Dfirst; tab[2, g] = Dlast
        if nseq == 1:
            tab[3, g] = Dmid; tab[4, g] = Dmid; tab[6, g] = Pmid; tab[8, g] = Nmid
        else:
            tab[3, g] = Dlast; tab[4, g] = Dfirst
        tab[5, g] = Pmid; tab[7, g] = Nmid
    return tab


_CB = {}
_off = 0
for _n, _w in (("Fb", 256), ("Fneg", 256), ("SelR", 512), ("SelI", 512), ("T1", 64), ("T2", 64),
               ("Cb", 128), ("Sbn", 128), ("FK", 1024), ("ident", 128), ("ones", 128), ("pool", 36 * 128)):
    _CB[_n] = (_off, _w)
    _off += _w
NCB = _off


def host_consts(L, nseq):
    K = conv_consts(L, nseq)
    cb = np.zeros((128, NCB), np.float64)

    def put(name, arr):
        o, w = _CB[name]
        assert arr.shape == (128, w), (name, arr.shape)
        cb[:, o:o + w] = arr
    put("Fb", K["Fb"]); put("Fneg", K["Fneg"])
    put("SelR", np.concatenate(list(K["SelR"]), axis=1))
    put("SelI", np.concatenate(list(K["SelI"]), axis=1))
    put("T1", K["T1"]); put("T2", K["T2"]); put("Cb", K["Cb"]); put("Sbn", K["Sbn"])
    put("FK", np.concatenate([K["FK"][k, j] for k in range(4) for j in range(2)], axis=1))
    put("ident", np.eye(128)); put("ones", np.ones((128, 128)))
    pm = pool_mats(L, nseq)
    put("pool", np.concatenate([pm[k, g] for k in range(9) for g in range(4)], axis=1))
    lag, valid = lag_table(L)
    lagc = np.clip(lag, 0, L - 1).astype(np.float64)
    t = lagc / (L - 1)
    w = (2.0 * np.pi / L) * lagc
    fb = np.linspace(1e-4, 15.0, 16)
    z = np.concatenate([t[..., None], np.cos(fb * w[..., None]), -np.sin(fb * w[..., None])], axis=-1)
    z = z * valid[..., None]
    zT = z.reshape(64 * 128, 33).T.copy()
    vmask = valid.reshape(1, 8192).astype(np.float32).astype(NPBF)
    negt = (-(t * valid)).T.copy()
    gm = np.full((128, 1), 1.0 if nseq == 1 else 0.0, np.float32)
    return dict(cb16=cb.astype(np.float32), zT=zT.astype(np.float32), vmask=vmask,
                negt=negt.astype(np.float32), gm=gm)


def MM(out, lhsT, rhs, start=True, stop=True):
    return lambda e: e.matmul(out, lhsT=lhsT, rhs=rhs, start=start, stop=stop)


def TR(out, in_, ident):
    return lambda e: e.transpose(out, in_, ident)


def ACT(out, in_, func, bias=None, scale=None):
    kw = {}
    if bias is not None:
        kw["bias"] = bias
    if scale is not None:
        kw["scale"] = scale
    return lambda e: e.activation(out=out, in_=in_, func=func, **kw)


def TT(out, in0, in1, op):
    return lambda e: e.tensor_tensor(out=out, in0=in0, in1=in1, op=op)


def TS(out, in0, s1, op0, s2=None, op1=None):
    if op1 is None:
        return lambda e: e.tensor_scalar(out=out, in0=in0, scalar1=s1, scalar2=None, op0=op0)
    return lambda e: e.tensor_scalar(out=out, in0=in0, scalar1=s1, scalar2=s2, op0=op0, op1=op1)


def STT(out, in0, scalar, in1, op0, op1):
    return lambda e: e.scalar_tensor_tensor(out=out, in0=in0, scalar=scalar, in1=in1, op0=op0, op1=op1)


def CP(out, in_):
    return lambda e: e.tensor_copy(out=out, in_=in_)


def MSET(ap, v):
    return lambda e: e.memset(ap, v)


def DMA(out, in_, slow=False):
    if slow:
        return lambda e: e.dma_start(out=out, in_=in_, allow_slow_non_contiguous=True)
    return lambda e: e.dma_start(out=out, in_=in_)


class Arena:
    def __init__(self, t, nwords):
        self.t = t
        self.n = nwords
        self.off = 0

    def f32(self, cols, parts=128):
        assert self.off + cols <= self.n, ("arena overflow", self.off, cols, self.n)
        ap = self.t[0:parts, self.off:self.off + cols]
        self.off += cols
        return ap

    def bf16(self, cols, parts=128):
        w = (cols + 1) // 2
        assert self.off + w <= self.n, ("arena overflow", self.off, w, self.n)
        ap = self.t[0:parts, self.off:self.off + w].bitcast(BF16)
        self.off += w
        return ap[:, 0:cols]


class PsPool:
    def __init__(self, tens):
        self.t = tens
        self.b = [Buf("ps%d" % i, excl=True) for i in range(len(tens))]
        self.busy = [False] * len(tens)
        self.nxt = 0

    def get(self):
        n = len(self.t)
        for k in range(n):
            i = (self.nxt + k) % n
            if not self.busy[i]:
                self.busy[i] = True
                self.nxt = (i + 1) % n
                return i
        raise RuntimeError("no free psum bank")

    def rel(self, i):
        self.busy[i] = False


def _barrier(P):
    targets = [(k, P.cnt[k]) for k in P.ENGS if P.cnt[k] > 0]
    targets += [(k, v) for k, v in P.dma_cnt.items() if v > 0]
    for eng in P.ENGS:
        waits = []
        for k, v in targets:
            if k == eng:
                continue
            if P.seen[eng].get(k, 0) < v:
                P.seen[eng][k] = v
                waits.append((k, v))
        P.items[eng].append((waits, None, None))


INPUT_SPECS = [
    ("x", [T, D], F32),
    ("w_in", [D, 5632], F32), ("b_in", [1, 5632], F32),
    ("w_pool", [512, 128], F32), ("b_pool", [1, 512], F32), ("pool_scale", [1, 512], F32),
    ("w_pool_proj", [512, D], F32),
    ("conv_w", [3, 3072], F32), ("conv_b", [1, 3072], F32),
    ("w_f1", [33, 64], F32), ("b_f1", [64, 1], F32), ("freq_f1", [64, 1], F32),
    ("w_f2", [64, 64], F32), ("b_f2", [64, 1], F32), ("freq_f2", [64, 1], F32),
    ("w_f_out", [64, 4096], F32), ("decay_rate", [4, HW], F32), ("hyena_bias", [1, 2 * HW], F32),
    ("w_hyena_proj", [HW, D], F32), ("w_o", [D, D], F32), ("b_o", [1, D], F32),
    ("ln1_g", [1, D], F32), ("ln1_b", [1, D], F32),
    ("w_ff1", [D, DFF], F32), ("b_ff1", [1, DFF], F32), ("w_ff2", [DFF, D], F32), ("b_ff2", [1, D], F32),
    ("ln2_g", [1, D], F32), ("ln2_b", [1, D], F32), ("ln_in_g", [1, D], F32), ("ln_in_b", [1, D], F32),
    ("cb16", [128, NCB], F32), ("zT", [33, 8192], F32), ("vmask", [1, 8192], BF16),
    ("negt", [128, 64], F32), ("gm", [128, 1], F32),
]
NCA = _CB["pool"][0]
ARENA_WORDS = 34400


def build_program(ngroups=8, ntiles=8, dbg=False, phase2=True):
    nc = bass.Bass("TRN2", target_bir_lowering=False)
    I = {}
    for name, shape, dt in INPUT_SPECS:
        I[name] = nc.dram_tensor(name, list(shape), dt, kind="ExternalInput").ap()
    y_out = nc.dram_tensor("y", [T, D], F32, kind="ExternalOutput").ap()
    z2T_d = nc.dram_tensor("z2T_d", [HW, T], BF16, kind=("ExternalOutput" if dbg else "Internal")).ap()
    scr = {}
    for name, shape in (("wa_s", [D, 512]), ("wgt_s", [16, 128, 8, 128]), ("wpool_s", [512, 128]),
                        ("wpp_s", [8, 128, 4, 128]), ("whp_s", [8, 128, 8, 128]), ("wo_s", [D, D]),
                        ("wf1_s", [32, 128, 8, 128]), ("wf2_s", [DFF, D])):
        scr[name] = nc.dram_tensor(name, shape, BF16, kind="Internal").ap()
    if dbg:
        hT_dbg = nc.dram_tensor("hT_dbg", [128, 8 * T], BF16, kind="ExternalOutput").ap()
        hid_dbg = nc.dram_tensor("hid_dbg", [64, 8192], BF16, kind="ExternalOutput").ap()

    with ExitStack() as es:
        P = Prog(nc, es)

        def sb(name, shape, dt):
            return es.enter_context(nc.sbuf_tensor("sb_" + name, list(shape), dt))

        cbt = sb("cbt", [128, NCA], BF16)
        identf = sb("identf", [128, 128], F32)
        hT = sb("hT", [128, 8 * T], BF16)
        colsH = sb("colsH", [128, 120], F32)
        colsG = sb("colsG", [128, 56], F32)
        bps = sb("bps", [128, 4], F32)
        fcol = sb("fcol", [128, 8], F32)
        gmt = sb("gmt", [128, 1], F32)
        negt = sb("negt", [128, 64], F32)
        arena_t = sb("arena", [128, ARENA_WORDS], F32)
        pst = [es.enter_context(nc.psum_tensor("ps%d" % i, [128, 512], F32)) for i in range(8)]
        PS = PsPool(pst)
        A = Arena(arena_t, ARENA_WORDS)

        def cst(name, j=0, w=None):
            o, ww = _CB[name]
            if w is None:
                return cbt[:, o:o + ww]
            return cbt[:, o + j * w:o + (j + 1) * w]
        identb = cst("ident")
        onesb = cst("ones")
        hT3 = hT[:].rearrange("p (dc t) -> p dc t", dc=8)

        Bc = Buf("consts")
        Bparam = Buf("params")
        BhT = Buf("hT")

        P.dma("pool", DMA(cbt[:], I["cb16"][:, 0:NCA]), writes=[Bc])
        P.dma("sp", DMA(gmt[:], I["gm"][:, :]), "ldc", writes=[Bparam])
        P.dma("sp", DMA(negt[:], I["negt"][:, :]), "ldc", writes=[Bparam])
        for j, nm in enumerate(("b_f1", "freq_f1", "b_f2", "freq_f2")):
            P.dma("sp", DMA(fcol[0:64, j:j + 1], I[nm][:, :]), "ldc", writes=[Bparam])
            P.dma("sp", DMA(fcol[64:128, j:j + 1], I[nm][:, :]), "ldc", writes=[Bparam])
        P.op("act", ACT(identf[:], identb, AF.Copy), reads=[Bc], writes=[Bparam])

        hid2T = A.bf16(4096)
        wfo = A.bf16(4096)
        Bhid = Buf("hid2T")
        Bwfo = Buf("wfo")
        Bwfo2 = Buf("wfo_hi")
        P.dma("pool", DMA(wfo[0:64, :], I["w_f_out"][:, :]), writes=[Bwfo])
        P.dma("pool", DMA(wfo[64:128, :], I["w_f_out"][:, :]), writes=[Bwfo2])
        mark1 = A.off

        rowsH = A.f32(128, parts=120)
        rowsG = A.f32(128, parts=56)
        Brow = Buf("rows")
        binh = I["b_in"][0, 512:3584].rearrange("(r p) -> r p", p=128)
        P.dma("sp", DMA(rowsH[0:24, :], binh), "ldc", writes=[Brow])
        for k in range(3):
            P.dma("sp", DMA(rowsH[24 + 24 * k:48 + 24 * k, :], I["conv_w"][k, :].rearrange("(r p) -> r p", p=128)),
                  "ldc", writes=[Brow])
        P.dma("sp", DMA(rowsH[96:120, :], I["conv_b"][0, :].rearrange("(r p) -> r p", p=128)), "ldc", writes=[Brow])
        P.dma("sp", DMA(rowsG[0:16, :], I["b_in"][0, 3584:5632].rearrange("(r p) -> r p", p=128)), "ldc", writes=[Brow])
        P.dma("sp", DMA(rowsG[16:20, :], I["b_pool"][0, :].rearrange("(r p) -> r p", p=128)), "ldc", writes=[Brow])
        P.dma("sp", DMA(rowsG[20:24, :], I["pool_scale"][0, :].rearrange("(r p) -> r p", p=128)), "ldc", writes=[Brow])
        P.dma("sp", DMA(rowsG[24:56, :], I["b_ff1"][0, :].rearrange("(r p) -> r p", p=128)), "ldc", writes=[Brow])
        b = PS.get()
        P.op("pe", MM(pst[b][:, 0:120], rowsH[0:120, :], identf[0:120, 0:120]), reads=[Brow, Bparam], writes=[PS.b[b]])
        P.op("pe", MM(pst[b][:, 128:184], rowsG[0:56, :], identf[0:56, 0:56]), reads=[Brow, Bparam], writes=[PS.b[b]])
        P.op("dve", CP(colsH[:], pst[b][:, 0:120]), reads=[PS.b[b]], writes=[Bparam])
        P.op("dve", CP(colsG[:], pst[b][:, 128:184]), reads=[PS.b[b]], writes=[Bparam])
        PS.rel(b)
        P.op("dve", TT(bps[:], colsG[:, 16:20], colsG[:, 20:24], ALU.mult), reads=[Bparam], writes=[Bparam])
        fc2 = sb("fc2", [128, 4], F32)
        for j in range(2):
            P.op("dve", TS(fc2[:, 2 * j:2 * j + 1], fcol[:, 2 * j + 1:2 * j + 2], 1.0 / TWO_PI, ALU.mult),
                 reads=[Bparam], writes=[Bparam])
            P.op("dve", TT(fc2[:, 2 * j + 1:2 * j + 2], fc2[:, 2 * j:2 * j + 1], fcol[:, 2 * j:2 * j + 1], ALU.mult),
                 reads=[Bparam], writes=[Bparam])

        zTt = A.f32(8192, parts=33)
        w1t = A.f32(128, parts=33)
        w2t = A.f32(128, parts=64)
        vmb = A.bf16(8192)
        h1c = [A.f32(512) for _ in range(2)]
        tq = [A.f32(512) for _ in range(2)]
        kq = [A.f32(512) for _ in range(2)]
        Bz = Buf("zT"); Bh1c = [Buf("h1c0"), Buf("h1c1")]
        Btq = [Buf("tq0"), Buf("tq1")]; Bkq = [Buf("kq0"), Buf("kq1")]
        P.dma("sp", DMA(zTt, I["zT"][:, :]), "ldz", writes=[Bz])
        for hh in range(2):
            P.dma("sp", DMA(w1t[:, hh * 64:(hh + 1) * 64], I["w_f1"][:, :]), "ldz", writes=[Bz])
            P.dma("sp", DMA(w2t[:, hh * 64:(hh + 1) * 64], I["w_f2"][:, :]), "ldz", writes=[Bz])
        P.dma("sp", DMA(vmb, I["vmask"][0, :].partition_broadcast(128)), "ldz", writes=[Bz])
        for ch in range(16):
            cs = slice(ch * 512, (ch + 1) * 512)
            k = ch % 2
            for layer in range(2):
                b = PS.get()
                if layer == 0:
                    P.op("pe", MM(pst[b][:, :], w1t[0:33, :], zTt[0:33, cs]), reads=[Bz], writes=[PS.b[b]])
                else:
                    P.op("pe", MM(pst[b][:, :], w2t[0:64, :], h1c[k][0:64, :]), reads=[Bz, Bh1c[k]], writes=[PS.b[b]])
                P.op("act", ACT(tq[k], pst[b][:, :], AF.Identity, bias=fc2[:, 2 * layer + 1:2 * layer + 2],
                                scale=fc2[:, 2 * layer:2 * layer + 1]), reads=[PS.b[b], Bparam], writes=[Btq[k]])
                PS.rel(b)
                P.op("dve", TS(kq[k], tq[k], MAGIC, ALU.add), reads=[Btq[k]], writes=[Bkq[k]])
                P.op("dve", TS(kq[k], kq[k], -MAGIC, ALU.add), reads=[Bkq[k]], writes=[Bkq[k]])
                P.op("dve", TT(kq[k], tq[k], kq[k], ALU.subtract), reads=[Btq[k], Bkq[k]], writes=[Bkq[k]])
                if layer == 0:
                    P.op("act", ACT(h1c[k], kq[k], AF.Sin, scale=TWO_PI), reads=[Bkq[k]], writes=[Bh1c[k]])
                else:
                    P.op("act", ACT(tq[k], kq[k], AF.Sin, scale=TWO_PI), reads=[Bkq[k]], writes=[Btq[k]])
                    if ch < 8:
                        P.op("dve", TT(hid2T[0:64, cs], tq[k][0:64, :], vmb[0:64, cs], ALU.mult), reads=[Btq[k], Bz], writes=[Bhid])
                    else:
                        cs2 = slice((ch - 8) * 512, (ch - 7) * 512)
                        P.op("dve", TT(hid2T[64:128, cs2], tq[k][64:128, :], vmb[64:128, cs], ALU.mult),
                             reads=[Btq[k], Bz], writes=[Bhid])

        lng = A.f32(1024)
        lnb = A.f32(1024)
        Bln = Buf("lnrows")
        P.dma("sp", DMA(lng, I["ln_in_g"][0, :].partition_broadcast(128)), "ldz", writes=[Bln])
        P.dma("sp", DMA(lnb, I["ln_in_b"][0, :].partition_broadcast(128)), "ldz", writes=[Bln])
        xb = [A.f32(1024) for _ in range(3)]
        Bxb = [Buf("xb%d" % i) for i in range(3)]
        nb_ = [A.f32(1024) for _ in range(2)]
        Bnb = [Buf("nb%d" % i) for i in range(2)]
        hb_ = [A.bf16(1024) for _ in range(2)]
        Bhb = [Buf("hb%d" % i) for i in range(2)]
        stt = [A.f32(16) for _ in range(3)]
        Bst = [Buf("st%d" % i) for i in range(3)]

        def layer_norm(xin, Bxin, st, Bs, out_n, Bout):
            P.op("dve", lambda e: e.bn_stats(out=st[:, 0:6], in_=xin[:, 0:512]), reads=[Bxin], writes=[Bs])
            P.op("dve", lambda e: e.bn_stats(out=st[:, 6:12], in_=xin[:, 512:1024]), reads=[Bxin], writes=[Bs])
            P.op("dve", lambda e: e.bn_aggr(out=st[:, 12:14], in_=st[:, 0:12]), reads=[Bs], writes=[Bs])
            P.op("act", ACT(st[:, 14:15], st[:, 13:14], AF.Sqrt, bias=epsc[:, 0:1]), reads=[Bs, Bparam], writes=[Bs])
            P.op("dve", lambda e: e.reciprocal(out=st[:, 14:15], in_=st[:, 14:15]), reads=[Bs], writes=[Bs])
            P.op("dve", STT(st[:, 15:16], st[:, 12:13], -1.0, st[:, 14:15], ALU.mult, ALU.mult), reads=[Bs], writes=[Bs])
            P.op("act", ACT(out_n, xin, AF.Identity, bias=st[:, 15:16], scale=st[:, 14:15]), reads=[Bxin, Bs], writes=[Bout])

        epsc = sb("epsc", [128, 1], F32)
        P.op("dve", MSET(epsc[:], LN_EPS), writes=[Bparam])

        for i in range(NSUB):
            k3 = i % 3; k2 = i % 2
            P.dma("sp", DMA(xb[k3], I["x"][i * 128:(i + 1) * 128, :]), "ldx%d" % k3, writes=[Bxb[k3]])
            layer_norm(xb[k3], Bxb[k3], stt[k3], Bst[k3], nb_[k2], Bnb[k2])
            P.op("pool", TT(nb_[k2], nb_[k2], lng, ALU.mult), reads=[Bnb[k2], Bln], writes=[Bnb[k2]])
            P.op("dve", TT(hb_[k2], nb_[k2], lnb, ALU.add), reads=[Bnb[k2], Bln], writes=[Bhb[k2]])
            b = PS.get()
            psb = pst[b][:].bitcast(BF16)
            for dc in range(8):
                P.op("pe", TR(psb[:, dc * 128:(dc + 1) * 128], hb_[k2][:, dc * 128:(dc + 1) * 128], identb),
                     reads=[Bhb[k2], Bc], writes=[PS.b[b]])
            src = psb[:, 0:1024].rearrange("p (dc t) -> p dc t", dc=8)
            dst = hT3[:, :, i * 128:(i + 1) * 128]
            P.op("act" if i % 2 == 0 else "dve", CP(dst, src) if i % 2 else ACT(dst, src, AF.Copy),
                 reads=[PS.b[b]], writes=[BhT])
            PS.rel(b)
        if dbg:
            P.dma("sp", DMA(hT_dbg[:, :], hT[:]), "dbg", reads=[BhT])
        _barrier(P)
        A.off = mark1
        st_ = dict(nc=nc, P=P, I=I, A=A, PS=PS, pst=pst, cst=cst, identb=identb, onesb=onesb, hT3=hT3, hT=hT,
                   colsH=colsH, colsG=colsG, bps=bps, gmt=gmt, negt=negt,
                   hid2T=hid2T, wfo=wfo, Bc=Bc, Bparam=Bparam, BhT=BhT, Bhid=Bhid, Bwfo=Bwfo, Bwfo2=Bwfo2,
                   z2T_d=z2T_d, scr=scr, y_out=y_out, layer_norm=layer_norm, identf=identf)
        Bz2 = Buf("z2T_d")
        st_["Bz2"] = Bz2
        dumped = set()

        def dump(name, ap, bufs, dt=BF16):
            if not dbg or name in dumped:
                return
            dumped.add(name)
            d = nc.dram_tensor(name, list(ap.shape), dt, kind="ExternalOutput").ap()
            P.dma("sp", DMA(d, ap), "dbg", reads=bufs)
        st_["dump"] = dump
        _phase1(st_, ngroups)
        fin = [Bz2]
        if phase2:
            _barrier(P)
            A.off = 0
            fin = _phase2(st_, ntiles)
        P.final_wait("sp", fin)
        P.emit()
    return nc


def _cast_jobs(st_):
    I, scr = st_["I"], st_["scr"]
    jobs = []

    def tiled(dst, src, kc, c0, nch):
        rs = slice(kc * 128, (kc + 1) * 128)
        jobs.append((dst[:, :, kc, :], src[rs, c0:c0 + nch * 128].rearrange("p (ch j) -> ch p j", j=128)))
    for r in range(8):
        tiled(scr["wgt_s"], I["w_in"], r, 3584, 16)
        tiled(scr["whp_s"], I["w_hyena_proj"], r, 0, 8)
        tiled(scr["wf1_s"], I["w_ff1"], r, 0, 32)
    for r in range(4):
        tiled(scr["wpp_s"], I["w_pool_proj"], r, 0, 8)
    jobs.append((scr["wa_s"][:, :], I["w_in"][:, 0:512]))
    jobs.append((scr["wpool_s"][:, :], I["w_pool"][:, :]))
    for r in range(2):
        rs = slice(r * 512, (r + 1) * 512)
        jobs.append((scr["wo_s"][rs, :], I["w_o"][rs, :]))
    for r in range(4):
        rs = slice(r * 1024, (r + 1) * 1024)
        jobs.append((scr["wf2_s"][rs, :], I["w_ff2"][rs, :]))
    return jobs


def _phase1(st_, ngroups):
    P, I, A, PS, pst, cst = st_["P"], st_["I"], st_["A"], st_["PS"], st_["pst"], st_["cst"]
    hT3, colsH, gmt, negt = st_["hT3"], st_["colsH"], st_["gmt"], st_["negt"]
    hid2T, wfo, identb, onesb, identf = st_["hid2T"], st_["wfo"], st_["identb"], st_["onesb"], st_["identf"]
    Bc, Bparam, BhT, Bhid, Bwfo, Bz2 = st_["Bc"], st_["Bparam"], st_["BhT"], st_["Bhid"], st_["Bwfo"], st_["Bz2"]
    Bwfo2 = st_["Bwfo2"]
    z2T_d = st_["z2T_d"]
    dump = st_["dump"]
    Bscr = Buf("scratchw")
    st_["Bscr"] = Bscr
    jobs = _cast_jobs(st_)
    jpos = 0

    wg = A.bf16(3 * 8 * 128); Bwg = [Buf("wg%d" % i) for i in range(3)]
    wg4 = wg.rearrange("p (t dc j) -> p t dc j", t=3, dc=8)
    absdec_o = [A.f32(256), A.f32(256)]; Bdec_o = [Buf("absdec0"), Buf("absdec1")]
    hbg_o = [A.f32(128, parts=1), A.f32(128, parts=1)]; Bhbg_o = [Buf("hbg0"), Buf("hbg1")]
    us = [A.f32(2052), None]
    t1 = A.f32(1024); Bt1 = Buf("t1")
    t1h = [t1[:, 0:512], t1[:, 512:1024]]; Bt1h = [Buf("t1h0"), Buf("t1h1")]
    for bb in Bt1h:
        bb.al.append(Bt1); Bt1.al.append(bb)
    ucT = A.bf16(4096); BucT = Buf("ucT")
    v_tm = A.bf16(4096); Bv = Buf("v_tm")
    x1_tm = A.bf16(4096); Bx1 = Buf("x1_tm")
    qbuf = [A.bf16(8193), A.bf16(8194)]; Bqs = [Buf("qA"), Buf("qB")]
    Wt = [A.f32(512) for _ in range(2)]; BWt = [[Buf("Wt%d_%d" % (i, k)) for k in range(4)] for i in range(2)]
    qtmp = [A.f32(512) for _ in range(2)]; Bqt = [Buf("qt0"), Buf("qt1")]
    nrmcol = A.f32(2); wsc = A.f32(4); Bwsc = Buf("wsc")
    hrow = A.f32(128, parts=1)
    qabs = [A.bf16(512) for _ in range(2)]; Bqa = [Buf("qa0"), Buf("qa1")]
    nfull = A.f32(512); nrm = A.f32(128); Bnrm = Buf("nrm")
    KABs = [A.bf16(4096), ucT]; BKs = [Buf("KAB0"), Buf("KAB1")]
    BKs[1].al.append(BucT); BucT.al.append(BKs[1])
    Asb = [A.bf16(512) for _ in range(4)]; BAsb = [Buf("Asb%d" % i) for i in range(4)]
    t1b = t1.bitcast(BF16)
    P1s = [A.bf16(1024), t1b[:, 0:1024]]; P2s = [A.bf16(1024), t1b[:, 1024:2048]]
    BP1s = [[Buf("P1_%d_%d" % (i, k)) for k in range(2)] for i in range(2)]
    BP2s = [[Buf("P2_%d_%d" % (i, k)) for k in range(2)] for i in range(2)]
    for bb in BP1s[1]:
        bb.al.append(Bt1h[0]); Bt1h[0].al.append(bb)
    for bb in BP2s[1]:
        bb.al.append(Bt1h[1]); Bt1h[1].al.append(bb)
    Zt = [A.bf16(1024) for _ in range(2)]; BZt = [Buf("Zt0"), Buf("Zt1")]
    z2o = [qbuf[1][:, 6146:7170], qbuf[1][:, 7170:8194]]; Bz2o = [Buf("z2o0"), Buf("z2o1")]
    for bb in Bz2o:
        bb.al.append(Bqs[1]); Bqs[1].al.append(bb)

    us[1] = A.f32(2052)
    Bub = [[Buf("u%d_%d" % (s_, k)) for k in range(4)] for s_ in range(2)]
    Buh = [[Buf("u%d_hl" % s_), Buf("u%d_hr" % s_)] for s_ in range(2)]
    Fb = cst("Fb"); Fneg = cst("Fneg"); T1c = cst("T1"); T2c = cst("T2"); Cb = cst("Cb"); Sbn = cst("Sbn")
    SelR = [cst("SelR", j, 128) for j in range(4)]
    SelI = [cst("SelI", j, 128) for j in range(4)]
    FK = [[cst("FK", k * 2 + j, 128) for j in range(2)] for k in range(4)]

    for o_ in range(2):
        P.op("pool", MSET(qbuf[o_][:, 0:1], 0.0), writes=[Bqs[o_]])
    qcs = [qb[:, 1:8193].rearrange("p (c r) -> p c r", r=64) for qb in qbuf]
    qvs = [qb[:, 1:8193].rearrange("p (c r) -> p r c", r=64) for qb in qbuf]
    vtm3 = v_tm.rearrange("p (c a) -> p a c", a=32)
    x1tm3 = x1_tm.rearrange("p (c a) -> p a c", a=32)
    asb_i = [0, 0]

    def inproj_conv(G, tt, ncol=None):
        bcol = colsH[:, tt * 8 + G:tt * 8 + G + 1]
        w0 = colsH[:, 24 + tt * 8 + G:24 + tt * 8 + G + 1]
        w1 = colsH[:, 48 + tt * 8 + G:48 + tt * 8 + G + 1]
        w2 = colsH[:, 72 + tt * 8 + G:72 + tt * 8 + G + 1]
        cbc = colsH[:, 96 + tt * 8 + G:96 + tt * 8 + G + 1]
        if ncol is not None:
            for k, src in enumerate((w0, w1, w2, cbc)):
                P.op("dve", TT(wsc[:, k:k + 1], src, ncol, ALU.mult), reads=[Bparam, Bnrm] + Bt1h + [BucT], writes=[Bwsc])
            w0, w1, w2, cbc = (wsc[:, k:k + 1] for k in range(4))
        cnt = [0]

        def conv_block(s, k):
            u = us[s]
            base = 1 + k * 512
            th_ = cnt[0] % 2
            cnt[0] += 1
            rds = [Bub[s][k], Bub[s][k - 1] if k > 0 else Buh[s][0], Bub[s][k + 1] if k < 3 else Buh[s][1], Bparam, Bwsc]
            P.op("act", ACT(t1h[th_], u[:, base:base + 512], AF.Identity, bias=cbc, scale=w1), reads=rds, writes=[Bt1h[th_]])
            P.op("dve", STT(t1h[th_], u[:, base - 1:base + 511], w0, t1h[th_], ALU.mult, ALU.add),
                 reads=rds + [Bt1h[th_]], writes=[Bt1h[th_]])
            o0 = s * 2048 + k * 512
            P.op("dve", STT(ucT[:, o0:o0 + 512], u[:, base + 1:base + 513], w2, t1h[th_], ALU.mult, ALU.add),
                 reads=rds + [Bt1h[th_]], writes=[BucT])

        for s in range(2):
            u = us[s]
            th = 2048 if s == 0 else 2047
            hc = 2049 if s == 0 else 0
            zc = 0 if s == 0 else 2049
            Bhc = Buh[s][1] if s == 0 else Buh[s][0]
            Bzc = Buh[s][0] if s == 0 else Buh[s][1]
            b = PS.get()
            for dc in range(8):
                P.op("pe", MM(pst[b][:, 0:1], wg4[:, tt, dc, :], hT3[:, dc, th:th + 1], dc == 0, dc == 7),
                     reads=[Bwg[tt], BhT], writes=[PS.b[b]])
            P.op("act", ACT(u[:, hc:hc + 1], pst[b][:, 0:1], AF.Identity, bias=bcol),
                 reads=[PS.b[b], Bparam], writes=[Bhc])
            PS.rel(b)
            P.op("dve", TS(u[:, hc:hc + 1], u[:, hc:hc + 1], gmt[:, 0:1], ALU.mult), reads=[Bhc, Bparam], writes=[Bhc])
            P.op("dve", MSET(u[:, zc:zc + 1], 0.0), writes=[Bzc])
            for k in range(4):
                b = PS.get()
                t0 = s * 2048 + k * 512
                for dc in range(8):
                    P.op("pe", MM(pst[b][:, 0:512], wg4[:, tt, dc, :], hT3[:, dc, t0:t0 + 512], dc == 0, dc == 7),
                         reads=[Bwg[tt], BhT], writes=[PS.b[b]])
                P.op("act", ACT(u[:, 1 + k * 512:1 + (k + 1) * 512], pst[b][:, 0:512], AF.Identity, bias=bcol),
                     reads=[PS.b[b], Bparam], writes=[Bub[s][k]])
                PS.rel(b)
                if k > 0:
                    conv_block(s, k - 1)
            conv_block(s, 3)

    def to_time_major(dst3, Bdst):
        for a0 in range(0, 32, 8):
            b = PS.get()
            psb = pst[b][:].bitcast(BF16)
            for j in range(8):
                a = a0 + j
                P.op("pe", TR(psb[:, j * 128:(j + 1) * 128], ucT[:, a * 128:(a + 1) * 128], identb),
                     reads=[BucT, Bc], writes=[PS.b[b]])
            src = psb[:, 0:1024].rearrange("p (a c) -> p a c", a=8)
            eng = "act" if (a0 // 8) % 2 == 0 else "dve"
            P.op(eng, ACT(dst3[:, a0:a0 + 8, :], src, AF.Copy) if eng == "act" else CP(dst3[:, a0:a0 + 8, :], src),
                 reads=[PS.b[b]], writes=[Bdst])
            PS.rel(b)

    def filter_time(G, o):
        q, Bq, qc, qv = qbuf[o], Bqs[o], qcs[o], qvs[o]
        absdec, Bdec, hbg, Bhbg = absdec_o[o], Bdec_o[o], hbg_o[o], Bhbg_o[o]
        for k in range(2):
            P.dma("sp", DMA(absdec[:, k * 128:(k + 1) * 128],
                            I["decay_rate"][2 * o + k, G * 128:(G + 1) * 128].partition_broadcast(128)), writes=[Bdec])
        P.op("act", ACT(absdec, absdec, AF.Abs), reads=[Bdec], writes=[Bdec])
        P.dma("sp", DMA(hbg, I["hyena_bias"][0:1, o * HW + G * 128:o * HW + (G + 1) * 128]), writes=[Bhbg])
        pn = PS.get()
        for rb in range(16):
            dirn = 1 if rb < 8 else 0
            setc = (o * 2 + dirn)
            k2 = rb % 2
            b = PS.get()
            for k in range(4):
                r = rb * 4 + k
                if r < 32:
                    lhs, rhs, Bw_ = hid2T[0:64, r * 128:(r + 1) * 128], wfo[0:64, setc * 1024 + G * 128:setc * 1024 + (G + 1) * 128], Bwfo
                else:
                    lhs, rhs, Bw_ = (hid2T[64:128, (r - 32) * 128:(r - 31) * 128],
                                     wfo[64:128, setc * 1024 + G * 128:setc * 1024 + (G + 1) * 128], Bwfo2)
                P.op("pe", MM(pst[b][:, k * 128:(k + 1) * 128], lhs, rhs), reads=[Bhid, Bw_], writes=[PS.b[b]])
                P.op("act", ACT(Wt[k2][:, k * 128:(k + 1) * 128], absdec[:, dirn * 128:(dirn + 1) * 128], AF.Exp,
                                scale=negt[:, r:r + 1]), reads=[Bdec, Bparam], writes=[BWt[k2][k]])
            P.op("dve", TT(qtmp[k2], pst[b][:, 0:512], Wt[k2], ALU.mult), reads=[PS.b[b]] + BWt[k2], writes=[Bqt[k2]])
            PS.rel(b)
            P.op("dve", STT(qabs[k2], qtmp[k2], -1.0, qtmp[k2], ALU.mult, ALU.max), reads=[Bqt[k2]], writes=[Bqa[k2]])
            if rb > 0:
                P.op("pe", MM(pst[pn][:, 0:512], onesb, qabs[1 - k2], rb == 1, False), reads=[Bc, Bqa[1 - k2]], writes=[PS.b[pn]])
            P.op("pool", CP(qv[:, rb * 4:rb * 4 + 4, :], qtmp[k2].rearrange("p (k c) -> p k c", k=4)),
                 reads=[Bqt[k2]], writes=[Bq])
            yield
        yield
        P.op("pe", MM(pst[pn][:, 0:512], onesb, qabs[1], False, True), reads=[Bc, Bqa[1]], writes=[PS.b[pn]])
        P.op("act", ACT(nfull, pst[pn][:, 0:512], AF.Copy), reads=[PS.b[pn]], writes=[Bnrm])
        PS.rel(pn)
        P.op("dve", TT(nrm, nfull[:, 0:128], nfull[:, 128:256], ALU.add), reads=[Bnrm], writes=[Bnrm])
        P.op("dve", TT(nrm, nrm, nfull[:, 256:384], ALU.add), reads=[Bnrm], writes=[Bnrm])
        P.op("dve", TT(nrm, nrm, nfull[:, 384:512], ALU.add), reads=[Bnrm], writes=[Bnrm])
        P.op("dve", TS(nrm, nrm, 1e-6, ALU.add), reads=[Bnrm], writes=[Bnrm])
        P.op("dve", TT(hrow, nrm[0:1, :], hbg[0:1, :], ALU.mult), reads=[Bnrm, Bhbg], writes=[Bnrm])
        P.op("dve", TT(qc[0:1, :, 32], qc[0:1, :, 32], hrow, ALU.add), reads=[Bq, Bnrm], writes=[Bq])
        P.op("dve", lambda e: e.reciprocal(out=nrm, in_=nrm), reads=[Bnrm], writes=[Bnrm])
        yield
        yield
        b = PS.get()
        P.op("pe", MM(pst[b][:, 0:1], nrm[0:1, :], identf[0:1, 0:1]), reads=[Bnrm, Bparam], writes=[PS.b[b]])
        P.op("dve", CP(nrmcol[:, o:o + 1], pst[b][:, 0:1]), reads=[PS.b[b]], writes=[Bnrm])
        PS.rel(b)
        yield

    def filter_spectra(sg, kb, o):
        KAB, BK = KABs[kb], BKs[kb]
        q, Bq = qbuf[o], Bqs[o]

        def f2(qp):
            pa = PS.get()
            for pp in range(2):
                c0 = sg * 16 + qp * 4 + pp * 2
                half = pst[pa][:, pp * 256:pp * 256 + 256]
                P.op("pe", MM(half, q[:, 1 + c0 * 64:1 + c0 * 64 + 128], Fb, True, False), reads=[Bq, Bc], writes=[PS.b[pa]])
                P.op("pe", MM(half, q[:, c0 * 64:c0 * 64 + 128], Fneg, False, True), reads=[Bq, Bc], writes=[PS.b[pa]])
            ai = asb_i[0] % 2
            asb_i[0] += 1
            P.op("act", ACT(Asb[ai], pst[pa][:, 0:512], AF.Copy), reads=[PS.b[pa]], writes=[BAsb[ai]])
            PS.rel(pa)
            return ai

        def f3(qp, ai):
            pk = [PS.get(), PS.get()]
            for j in range(2):
                for kind in range(2):
                    for ri in range(2):
                        for pp in range(2):
                            out = pst[pk[pp]][:, (j * 2 + kind) * 128:(j * 2 + kind + 1) * 128]
                            P.op("pe", MM(out, FK[kind * 2 + ri][j], Asb[ai][:, pp * 256 + ri * 128:pp * 256 + ri * 128 + 128],
                                          ri == 0, ri == 1), reads=[Bc, BAsb[ai]], writes=[PS.b[pk[pp]]])
            for pp in range(2):
                pidx = qp * 2 + pp
                eng = "act" if pp == 0 else "dve"
                dst = KAB[:, pidx * 512:(pidx + 1) * 512]
                P.op(eng, ACT(dst, pst[pk[pp]][:, 0:512], AF.Copy) if eng == "act" else CP(dst, pst[pk[pp]][:, 0:512]),
                     reads=[PS.b[pk[pp]]], writes=[BK])
                PS.rel(pk[pp])

        ai_prev = f2(0)
        yield
        for qp in range(4):
            ai_next = f2(qp + 1) if qp + 1 < 4 else None
            if ai_next is not None:
                yield
            f3(qp, ai_prev)
            ai_prev = ai_next
            yield

    def conv_data(sg, o, kb):
        KAB, BK = KABs[kb], BKs[kb]
        KAB4 = KAB.rearrange("p (c k f) -> p c k f", k=2, f=128)
        src = v_tm if o == 0 else x1_tm
        Bsrc = Bv if o == 0 else Bx1
        zi = sg % 2

        def d1(oc):
            pa = PS.get()
            for qd in range(2):
                cq = sg * 16 + oc * 8 + qd * 4
                P.op("pe", MM(pst[pa][:, qd * 256:(qd + 1) * 256], src[:, cq * 32:cq * 32 + 128], Fb),
                     reads=[Bsrc, Bc], writes=[PS.b[pa]])
            ai = 2 + asb_i[1] % 2
            asb_i[1] += 1
            P.op("act", ACT(Asb[ai], pst[pa][:, 0:512], AF.Copy), reads=[PS.b[pa]], writes=[BAsb[ai]])
            PS.rel(pa)
            return ai

        def d2(oc, ai):
            pi = (sg * 2 + oc) % 2
            P1, P2, BP1, BP2 = P1s[pi], P2s[pi], BP1s[pi], BP2s[pi]
            pu = [PS.get(), PS.get()]
            for j in range(4):
                for part in range(2):
                    for qd in range(2):
                        P.op("pe", MM(pst[pu[qd]][:, j * 128:(j + 1) * 128], (SelR if part == 0 else SelI)[j],
                                      Asb[ai][:, qd * 256 + part * 128:qd * 256 + part * 128 + 128], part == 0, part == 1),
                             reads=[Bc, BAsb[ai]], writes=[PS.b[pu[qd]]])
            for qd in range(2):
                ch0 = oc * 8 + qd * 4
                uv = pst[pu[qd]][:, 0:512].rearrange("p (c f) -> p c f", c=4)
                P.op("dve", TT(P1[:, qd * 512:(qd + 1) * 512].rearrange("p (c f) -> p c f", c=4), uv,
                               KAB4[:, ch0:ch0 + 4, 0, :], ALU.mult), reads=[PS.b[pu[qd]], BK], writes=[BP1[qd]])
                P.op("dve", TT(P2[:, qd * 512:(qd + 1) * 512].rearrange("p (c f) -> p c f", c=4), uv,
                               KAB4[:, ch0:ch0 + 4, 1, :], ALU.mult), reads=[PS.b[pu[qd]], BK], writes=[BP2[qd]])
                PS.rel(pu[qd])

        def d4(oc):
            pi = (sg * 2 + oc) % 2
            P1, P2, BP1, BP2 = P1s[pi], P2s[pi], BP1s[pi], BP2s[pi]
            pz = PS.get()
            for ch in range(8):
                P.op("pe", MM(pst[pz][:, ch * 64:(ch + 1) * 64], P1[:, ch * 128:(ch + 1) * 128], T1c, True, False),
                     reads=[BP1[ch // 4], Bc], writes=[PS.b[pz]])
                P.op("pe", MM(pst[pz][:, ch * 64:(ch + 1) * 64], P2[:, ch * 128:(ch + 1) * 128], T2c, False, True),
                     reads=[BP2[ch // 4], Bc], writes=[PS.b[pz]])
            P.op("act", ACT(Zt[zi][:, oc * 512:(oc + 1) * 512], pst[pz][:, 0:512], AF.Copy),
                 reads=[PS.b[pz]], writes=[BZt[zi]])
            PS.rel(pz)

        a0 = d1(0)
        yield
        a1 = d1(1)
        yield
        d2(0, a0)
        yield
        d4(0)
        d2(1, a1)
        yield
        d4(1)
        yield
        Z4 = Zt[zi].rearrange("p (c ri a) -> p c ri a", ri=2, a=32)
        py = PS.get()
        P.op("pe", MM(pst[py][:, 0:512], Cb, Z4[:, :, 0, :], True, False), reads=[Bc, BZt[zi]], writes=[PS.b[py]])
        P.op("pe", MM(pst[py][:, 0:512], Sbn, Z4[:, :, 1, :], False, True), reads=[Bc, BZt[zi]], writes=[PS.b[py]])
        cs = slice(sg * 512, (sg + 1) * 512)
        if o == 0:
            P.op("dve", TT(x1_tm[:, cs], pst[py][:, 0:512], x1_tm[:, cs], ALU.mult), reads=[PS.b[py], Bx1], writes=[Bx1])
        else:
            P.op("act", ACT(v_tm[:, cs], pst[py][:, 0:512], AF.Copy), reads=[PS.b[py]], writes=[Bv])
        PS.rel(py)
        yield

    def conv_pass(o, extra=None):
        def chain(fn):
            for sg in range(8):
                yield from fn(sg)
        fgen = chain(lambda sg: filter_spectra(sg, sg % 2, o))
        dgen = chain(lambda sg: conv_data(sg, o, sg % 2))
        NF, ND, NX, DT = 8.0, 6.0, 20.0, 48.0
        fpos = dpos = xpos = 0
        fdone = ddone = False
        xdone = extra is None
        while not (fdone and ddone and xdone):
            if not fdone and (ddone or fpos / NF < dpos / ND + 1.0):
                try:
                    next(fgen); fpos += 1
                except StopIteration:
                    fdone = True
            elif not xdone and (ddone or xpos / NX <= dpos / DT):
                try:
                    next(extra); xpos += 1
                except StopIteration:
                    xdone = True
            elif not ddone:
                try:
                    next(dgen); dpos += 1
                except StopIteration:
                    ddone = True

    def drain(gen):
        for _ in gen:
            pass

    for G in range(ngroups):
        for tt, c0 in ((0, 512), (1, 1536), (2, 2560)):
            src = I["w_in"][:, c0 + G * 128:c0 + (G + 1) * 128].rearrange("(dc p) j -> p dc j", p=128)
            P.dma("pool", DMA(wg4[:, tt, :, :], src), writes=[Bwg[tt]])
        nj = (len(jobs) + ngroups - 1) // ngroups
        for dst, srcw in jobs[jpos:jpos + nj]:
            P.dma("pool", DMA(dst, srcw), writes=[Buf("scr%d" % jpos)])
            jpos += 1
        if G == 0:
            drain(filter_time(0, 0))
        inproj_conv(G, 0, nrmcol[:, 0:1])
        to_time_major(x1tm3, Bx1)
        inproj_conv(G, 2)
        to_time_major(vtm3, Bv)
        conv_pass(0, extra=filter_time(G, 1))
        conv_pass(1, extra=(filter_time(G + 1, 0) if G + 1 < ngroups else None))
        inproj_conv(G, 1, nrmcol[:, 1:2])
        for a0 in range(0, 32, 8):
            b = PS.get()
            psb = pst[b][:].bitcast(BF16)
            for j in range(8):
                P.op("pe", TR(psb[:, j * 128:(j + 1) * 128], vtm3[:, a0 + j, :], identb), reads=[Bv, Bc], writes=[PS.b[b]])
            k2 = (a0 // 8) % 2
            P.op("dve", TT(z2o[k2], psb[:, 0:1024], ucT[:, a0 * 128:a0 * 128 + 1024], ALU.mult),
                 reads=[PS.b[b], BucT], writes=[Bz2o[k2]])
            PS.rel(b)
            P.dma("sp", DMA(z2T_d[G * 128:(G + 1) * 128, a0 * 128:a0 * 128 + 1024], z2o[k2]), "stz",
                  reads=[Bz2o[k2]], writes=[Bz2])
    for dst, srcw in jobs[jpos:]:
        P.dma("pool", DMA(dst, srcw), writes=[Buf("scr%d" % jpos)])
        jpos += 1


def _phase2(st_, ntiles):
    nc, P, I, A, PS, pst, cst = st_["nc"], st_["P"], st_["I"], st_["A"], st_["PS"], st_["pst"], st_["cst"]
    hT3, colsG, bps, identb, identf = st_["hT3"], st_["colsG"], st_["bps"], st_["identb"], st_["identf"]
    Bc, Bparam, BhT = st_["Bc"], st_["Bparam"], st_["BhT"]
    z2T_d, scr, y_out, layer_norm, dump = st_["z2T_d"], st_["scr"], st_["y_out"], st_["layer_norm"], st_["dump"]
    NSLOT, LA = 12, 8

    cbP = A.bf16(36 * 128); BcbP = Buf("cbP")
    rows = [A.f32(1024) for _ in range(6)]
    Brows = Buf("lnrows2")
    ring = [A.bf16(1024) for _ in range(NSLOT)]; Bring = [Buf("ring%d" % i) for i in range(NSLOT)]
    wpool = A.bf16(512); Bwpool = Buf("wpool")
    lcols = A.f32(16); Blc = Buf("lcols")
    h_tok = [A.f32(1024) for _ in range(4)]; Bh = [Buf("h_tok%d" % i) for i in range(4)]
    stt = [A.f32(16) for _ in range(4)]; Bst = [Buf("st2_%d" % i) for i in range(4)]
    n1bf = [A.bf16(1024) for _ in range(2)]; Bn1 = [Buf("n1bf0"), Buf("n1bf1")]
    h1T = A.bf16(8 * 512); Bh1T = Buf("h1T")
    rtmp = [A.f32(512) for _ in range(2)]; Brt = [Buf("rtmp0"), Buf("rtmp1")]
    xoff = A.off
    a_sub = [A.bf16(512) for _ in range(6)]; Ba = [Buf("a_sub%d" % i) for i in range(6)]
    pT = [A.bf16(512) for _ in range(4)]; BpT = [Buf("pT%d" % i) for i in range(4)]
    qT = [A.bf16(512) for _ in range(4)]; BqT = [Buf("qT%d" % i) for i in range(4)]
    z2t = A.bf16(8 * 512); Bz2t = Buf("z2t")
    gT = [A.bf16(512) for _ in range(4)]; BgT = [Buf("gT%d" % i) for i in range(4)]
    mt = [A.f32(512) for _ in range(2)]; Bmt = [Buf("mt0"), Buf("mt1")]
    mT = A.bf16(8 * 512); BmT = Buf("mT")
    xend = A.off
    A.off = xoff
    uT = A.bf16(32 * 512)
    BuT = [Buf("uT%d" % i) for i in range(32)]
    A.off = max(A.off, xend)
    mix = list(zip([xoff] * 0, []))
    def rng(ap_words_start, nwords):
        return (ap_words_start, ap_words_start + nwords)
    pos = xoff
    mixbufs = []
    for bl, nw in ([(b, 256) for b in Ba] + [(b, 256) for b in BpT] + [(b, 256) for b in BqT] + [(Bz2t, 2048)]
                   + [(b, 256) for b in BgT] + [(b, 512) for b in Bmt] + [(BmT, 2048)]):
        mixbufs.append((bl, pos, pos + nw))
        pos += nw
    assert pos == xend, (pos, xend)
    for k in range(32):
        lo, hi = xoff + k * 256, xoff + (k + 1) * 256
        for bl, a0, a1 in mixbufs:
            if a0 < hi and lo < a1:
                BuT[k].al.append(bl)
                bl.al.append(BuT[k])
    uT3 = uT.rearrange("p (fc t) -> p fc t", fc=32)
    h1T3 = h1T.rearrange("p (dc t) -> p dc t", dc=8)
    mT3 = mT.rearrange("p (dc t) -> p dc t", dc=8)
    z2t3 = z2t.rearrange("p (cc t) -> p cc t", cc=8)

    o_pool = _CB["pool"][0]
    P.dma("pool", DMA(cbP, I["cb16"][:, o_pool:o_pool + 36 * 128]), writes=[BcbP])
    P.dma("sp", DMA(wpool.rearrange("p (g d) -> p g d", g=4), scr["wpool_s"].rearrange("(g c) d -> c g d", c=128)),
          writes=[Bwpool])
    srcs = ("ln_in_g", "ln_in_b", "ln1_g", "ln1_b", "ln2_g", "ln2_b")
    for k, nm in enumerate(srcs):
        P.dma("sp", DMA(rows[k], I[nm][0, :].partition_broadcast(128)), writes=[Brows])
    tmpr = h_tok[0]
    P.dma("sp", DMA(tmpr, I["b_o"][0, :].partition_broadcast(128)), writes=[Bh[0]])
    P.op("dve", STT(rows[1], rows[1], ALPHA, tmpr, ALU.mult, ALU.add), reads=[Brows, Bh[0]], writes=[Brows])
    P.op("dve", TS(rows[0], rows[0], ALPHA, ALU.mult), reads=[Brows], writes=[Brows])
    rws = h_tok[1]
    P.dma("sp", DMA(rws[0:8, 0:128], I["ln1_g"][0, :].rearrange("(r p) -> r p", p=128)), writes=[Bh[1]])
    P.dma("sp", DMA(rws[8:16, 0:128], I["ln1_b"][0, :].rearrange("(r p) -> r p", p=128)), writes=[Bh[1]])
    b = PS.get()
    P.op("pe", MM(pst[b][:, 0:16], rws[0:16, 0:128], identf[0:16, 0:16]), reads=[Bh[1], Bparam], writes=[PS.b[b]])
    P.op("dve", CP(lcols, pst[b][:, 0:16]), reads=[PS.b[b]], writes=[Blc])
    PS.rel(b)
    tmp2 = h_tok[2]
    P.dma("sp", DMA(tmp2, I["b_ff2"][0, :].partition_broadcast(128)), writes=[Bh[2]])
    P.op("dve", STT(rows[3], rows[3], ALPHA, tmp2, ALU.mult, ALU.add), reads=[Brows, Bh[2]], writes=[Brows])
    P.op("dve", TS(rows[2], rows[2], ALPHA, ALU.mult), reads=[Brows], writes=[Brows])

    wa_v = scr["wa_s"].rearrange("(dc p) n -> p dc n", p=128)
    wo_v = scr["wo_s"].rearrange("(dc p) n -> p dc n", p=128)
    wf2_v = scr["wf2_s"].rearrange("(fc p) n -> p fc n", p=128)
    tile_items = []
    for i2 in range(4):
        tile_items.append(((2, 512), wa_v[:, 2 * i2:2 * i2 + 2, :]))
    for dmc in range(8):
        tile_items.append(((4, 128), scr["wpp_s"][dmc]))
        tile_items.append(((8, 128), scr["whp_s"][dmc]))
        tile_items.append(((8, 128), scr["wgt_s"][dmc]))
        tile_items.append(((8, 128), scr["wgt_s"][8 + dmc]))
    for half in range(2):
        for i2 in range(4):
            tile_items.append(((2, 512), wo_v[:, 2 * i2:2 * i2 + 2, half * 512:(half + 1) * 512]))
    for fc in range(32):
        tile_items.append(((8, 128), scr["wf1_s"][fc]))
    for half in range(2):
        for i2 in range(16):
            tile_items.append(((2, 512), wf2_v[:, 2 * i2:2 * i2 + 2, half * 512:(half + 1) * 512]))
    items = tile_items * ntiles
    wst = {"issue": 0, "use": 0}

    def wview(k):
        (a, bb), _ = items[k]
        return ring[k % NSLOT][:, 0:a * bb].rearrange("p (a b) -> p a b", a=a)

    def wget():
        while wst["issue"] < len(items) and wst["issue"] <= wst["use"] + LA:
            k = wst["issue"]
            P.dma("sp", DMA(wview(k), items[k][1]), writes=[Bring[k % NSLOT]])
            wst["issue"] += 1
        k = wst["use"]
        wst["use"] += 1
        return wview(k), Bring[k % NSLOT]

    def pool_blocks(i):
        if i == 0:
            dv = 1
        elif i == 31:
            dv = 2
        elif i == 15:
            dv = 3
        elif i == 16:
            dv = 4
        else:
            dv = 0
        pv = None if i == 0 else (6 if i == 16 else 5)
        nv = None if i == 31 else (8 if i == 15 else 7)
        return pv, dv, nv

    def pblk(k, g):
        o = (k * 4 + g) * 128
        return cbP[:, o:o + 128]

    def st_A(j):
        tsl = slice(j * 512, (j + 1) * 512)
        for sub in range(4):
            i = 4 * j + sub
            P.dma("sp", DMA(h_tok[sub], I["x"][i * 128:(i + 1) * 128, :]), writes=[Bh[sub]])
            layer_norm(h_tok[sub], Bh[sub], stt[sub], Bst[sub], h_tok[sub], Bh[sub])
            P.op("dve", TT(h_tok[sub], h_tok[sub], rows[0], ALU.mult), reads=[Bh[sub], Brows], writes=[Bh[sub]])
            P.op("pool", TT(h_tok[sub], h_tok[sub], rows[1], ALU.add), reads=[Bh[sub], Brows], writes=[Bh[sub]])

    def st_BE(j):
        tsl = slice(j * 512, (j + 1) * 512)
        wa = [wget() for _ in range(4)]
        for sl in range(6):
            i = min(max(4 * j - 1 + sl, 0), 31)
            b = PS.get()
            for dc in range(8):
                wv, Bw = wa[dc // 2]
                P.op("pe", MM(pst[b][:, 0:512], hT3[:, dc, i * 128:(i + 1) * 128], wv[:, dc % 2, :], dc == 0, dc == 7),
                     reads=[BhT, Bw], writes=[PS.b[b]])
            P.op("act", ACT(a_sub[sl], pst[b][:, 0:512], AF.Copy), reads=[PS.b[b]], writes=[Ba[sl]])
            PS.rel(b)
            yield
        for g in range(4):
            b = PS.get()
            for sub in range(4):
                i = 4 * j + sub
                pv, dv, nv = pool_blocks(i)
                lst = []
                if pv is not None:
                    lst.append((sub, pv))
                lst.append((sub + 1, dv))
                if nv is not None:
                    lst.append((sub + 2, nv))
                for n_, (sl, kb) in enumerate(lst):
                    P.op("pe", MM(pst[b][:, sub * 128:(sub + 1) * 128], a_sub[sl][:, g * 128:(g + 1) * 128], pblk(kb, g),
                                  n_ == 0, n_ == len(lst) - 1), reads=[Ba[sl], BcbP], writes=[PS.b[b]])
            P.op("dve", CP(pT[g], pst[b][:, 0:512]), reads=[PS.b[b]], writes=[BpT[g]])
            PS.rel(b)
            b = PS.get()
            P.op("pe", MM(pst[b][:, 0:512], wpool[:, g * 128:(g + 1) * 128], pT[g]), reads=[Bwpool, BpT[g]], writes=[PS.b[b]])
            P.op("act", ACT(qT[g], pst[b][:, 0:512], AF.Identity, bias=bps[:, g:g + 1], scale=colsG[:, 20 + g:21 + g]),
                 reads=[PS.b[b], Bparam], writes=[BqT[g]])
            PS.rel(b)
            yield
        P.dma("sp", DMA(z2t3, z2T_d[:, tsl].rearrange("(cc p) t -> p cc t", p=128)), writes=[Bz2t])
        for dmc in range(8):
            wv, Bw = wget()
            b1 = PS.get()
            for gc in range(4):
                P.op("pe", MM(pst[b1][:, 0:512], wv[:, gc, :], qT[gc], gc == 0, gc == 3), reads=[Bw, BqT[gc]], writes=[PS.b[b1]])
            wv, Bw = wget()
            b2 = PS.get()
            for cc in range(8):
                P.op("pe", MM(pst[b2][:, 0:512], wv[:, cc, :], z2t3[:, cc, :], cc == 0, cc == 7), reads=[Bw, Bz2t], writes=[PS.b[b2]])
            gi = []
            for br in range(2):
                wv, Bw = wget()
                b3 = PS.get()
                for dc in range(8):
                    P.op("pe", MM(pst[b3][:, 0:512], wv[:, dc, :], hT3[:, dc, tsl], dc == 0, dc == 7),
                         reads=[Bw, BhT], writes=[PS.b[b3]])
                k = (dmc * 2 + br) % 4
                P.op("act", ACT(gT[k], pst[b3][:, 0:512], AF.Sigmoid, bias=colsG[:, br * 8 + dmc:br * 8 + dmc + 1]),
                     reads=[PS.b[b3], Bparam], writes=[BgT[k]])
                PS.rel(b3)
                gi.append(k)
            P.op("dve", TT(mt[0], pst[b1][:, 0:512], gT[gi[0]], ALU.mult), reads=[PS.b[b1], BgT[gi[0]]], writes=[Bmt[0]])
            PS.rel(b1)
            P.op("dve", TT(mt[1], pst[b2][:, 0:512], gT[gi[1]], ALU.mult), reads=[PS.b[b2], BgT[gi[1]]], writes=[Bmt[1]])
            PS.rel(b2)
            P.op("pool", TT(mT3[:, dmc, :], mt[0], mt[1], ALU.add), reads=[Bmt[0], Bmt[1]], writes=[BmT])
            yield

    def st_FI(j):
        tsl = slice(j * 512, (j + 1) * 512)
        for half in range(2):
            hs = slice(half * 512, (half + 1) * 512)
            bk = [PS.get() for _ in range(4)]
            for i2 in range(4):
                wv, Bw = wget()
                for dd in range(2):
                    dc = 2 * i2 + dd
                    for sub in range(4):
                        P.op("pe", MM(pst[bk[sub]][:, 0:512], mT3[:, dc, sub * 128:(sub + 1) * 128], wv[:, dd, :],
                                      dc == 0, dc == 7), reads=[BmT, Bw], writes=[PS.b[bk[sub]]])
            for sub in range(4):
                P.op("dve", TT(h_tok[sub][:, hs], pst[bk[sub]][:, 0:512], h_tok[sub][:, hs], ALU.add),
                     reads=[PS.b[bk[sub]], Bh[sub]], writes=[Bh[sub]])
                PS.rel(bk[sub])
        for sub in range(4):
            k2 = sub % 2
            layer_norm(h_tok[sub], Bh[sub], stt[sub], Bst[sub], h_tok[sub], Bh[sub])
            P.op("act", ACT(n1bf[k2], h_tok[sub], AF.Copy), reads=[Bh[sub]], writes=[Bn1[k2]])
            b = PS.get()
            psb = pst[b][:].bitcast(BF16)
            for dc in range(8):
                P.op("pe", TR(psb[:, dc * 128:(dc + 1) * 128], n1bf[k2][:, dc * 128:(dc + 1) * 128], identb),
                     reads=[Bn1[k2], Bc], writes=[PS.b[b]])
            for dc in range(8):
                P.op("act", ACT(h1T3[:, dc, sub * 128:(sub + 1) * 128], psb[:, dc * 128:(dc + 1) * 128], AF.Identity,
                                bias=lcols[:, 8 + dc:9 + dc], scale=lcols[:, dc:dc + 1]),
                     reads=[PS.b[b], Blc], writes=[Bh1T])
            PS.rel(b)
        for fc in range(32):
            wv, Bw = wget()
            b = PS.get()
            for dc in range(8):
                P.op("pe", MM(pst[b][:, 0:512], wv[:, dc, :], h1T3[:, dc, :], dc == 0, dc == 7), reads=[Bw, Bh1T], writes=[PS.b[b]])
            k2 = fc % 2
            P.op("dve", TS(rtmp[k2], pst[b][:, 0:512], colsG[:, 24 + fc:25 + fc], ALU.add, 0.0, ALU.max),
                 reads=[PS.b[b], Bparam], writes=[Brt[k2]])
            PS.rel(b)
            P.op("act", ACT(uT3[:, fc, :], rtmp[k2], AF.Square), reads=[Brt[k2]], writes=[BuT[fc]])
        for sub in range(4):
            P.op("dve", TT(h_tok[sub], h_tok[sub], rows[2], ALU.mult), reads=[Bh[sub], Brows], writes=[Bh[sub]])
            P.op("dve", TT(h_tok[sub], h_tok[sub], rows[3], ALU.add), reads=[Bh[sub], Brows], writes=[Bh[sub]])
        for half in range(2):
            hs = slice(half * 512, (half + 1) * 512)
            bk = [PS.get() for _ in range(4)]
            for i2 in range(16):
                wv, Bw = wget()
                for dd in range(2):
                    fc = 2 * i2 + dd
                    for sub in range(4):
                        P.op("pe", MM(pst[bk[sub]][:, 0:512], uT3[:, fc, sub * 128:(sub + 1) * 128], wv[:, dd, :],
                                      fc == 0, fc == 31), reads=[BuT[fc], Bw], writes=[PS.b[bk[sub]]])
            for sub in range(4):
                P.op("dve", TT(h_tok[sub][:, hs], pst[bk[sub]][:, 0:512], h_tok[sub][:, hs], ALU.add),
                     reads=[PS.b[bk[sub]], Bh[sub]], writes=[Bh[sub]])
                PS.rel(bk[sub])

    def st_J(j):
        tsl = slice(j * 512, (j + 1) * 512)
        for sub in range(4):
            i = 4 * j + sub
            layer_norm(h_tok[sub], Bh[sub], stt[sub], Bst[sub], h_tok[sub], Bh[sub])
            P.op("dve", TT(h_tok[sub], h_tok[sub], rows[4], ALU.mult), reads=[Bh[sub], Brows], writes=[Bh[sub]])
            P.op("pool", TT(h_tok[sub], h_tok[sub], rows[5], ALU.add), reads=[Bh[sub], Brows], writes=[Bh[sub]])
            P.dma("act", DMA(y_out[i * 128:(i + 1) * 128, :], h_tok[sub]), reads=[Bh[sub]])

    def st_JA(j):
        for sub in range(4):
            i = 4 * j + sub
            layer_norm(h_tok[sub], Bh[sub], stt[sub], Bst[sub], h_tok[sub], Bh[sub])
            P.op("dve", TT(h_tok[sub], h_tok[sub], rows[4], ALU.mult), reads=[Bh[sub], Brows], writes=[Bh[sub]])
            P.op("pool", TT(h_tok[sub], h_tok[sub], rows[5], ALU.add), reads=[Bh[sub], Brows], writes=[Bh[sub]])
            P.dma("act", DMA(y_out[i * 128:(i + 1) * 128, :], h_tok[sub]), reads=[Bh[sub]])
            yield
            if j + 1 < ntiles:
                i2 = 4 * (j + 1) + sub
                P.dma("sp", DMA(h_tok[sub], I["x"][i2 * 128:(i2 + 1) * 128, :]), writes=[Bh[sub]])
                layer_norm(h_tok[sub], Bh[sub], stt[sub], Bst[sub], h_tok[sub], Bh[sub])
                P.op("dve", TT(h_tok[sub], h_tok[sub], rows[0], ALU.mult), reads=[Bh[sub], Brows], writes=[Bh[sub]])
                P.op("pool", TT(h_tok[sub], h_tok[sub], rows[1], ALU.add), reads=[Bh[sub], Brows], writes=[Bh[sub]])
                yield

    def interleave(g1, n1, g2, n2):
        p1 = p2 = 0
        d1 = d2 = False
        while not (d1 and d2):
            if not d1 and (d2 or p1 * n2 <= p2 * n1):
                try:
                    next(g1); p1 += 1
                except StopIteration:
                    d1 = True
            else:
                try:
                    next(g2); p2 += 1
                except StopIteration:
                    d2 = True

    st_A(0)
    for _ in st_BE(0):
        pass
    for j in range(ntiles):
        st_FI(j)
        if j + 1 < ntiles:
            interleave(st_JA(j), 8, st_BE(j + 1), 18)
        else:
            for _ in st_JA(j):
                pass
    return Bh


_HC = {}


def _host_consts_cached(L, nseq):
    key = (L, nseq)
    if key not in _HC:
        _HC[key] = host_consts(L, nseq)
    return _HC[key]


def make_in_maps(inputs):
    f = lambda a: np.ascontiguousarray(np.asarray(a), dtype=np.float32)
    p = {k: np.asarray(v) for k, v in inputs.items()}
    shared = {
        "w_in": f(p["w_in"][0]), "b_in": f(p["b_in"][0]).reshape(1, -1),
        "w_pool": f(p["w_pool"][0]).reshape(512, 128), "b_pool": f(p["b_pool"][0]).reshape(1, 512),
        "pool_scale": f(p["pool_scale"][0]).reshape(1, 512), "w_pool_proj": f(p["w_pool_proj"][0]),
        "conv_w": f(p["conv_w"][0]), "conv_b": f(p["conv_b"][0]).reshape(1, -1),
        "w_f1": f(p["w_f1"][0]), "b_f1": f(p["b_f1"][0]).reshape(64, 1), "freq_f1": f(p["freq_f1"][0]).reshape(64, 1),
        "w_f2": f(p["w_f2"][0]), "b_f2": f(p["b_f2"][0]).reshape(64, 1), "freq_f2": f(p["freq_f2"][0]).reshape(64, 1),
        "w_f_out": f(p["w_f_out"][0]), "decay_rate": f(p["decay_rate"][0]),
        "hyena_bias": f(p["hyena_bias"][0]).reshape(1, -1),
        "w_hyena_proj": f(p["w_hyena_proj"][0]), "w_o": f(p["w_o"][0]), "b_o": f(p["b_o"][0]).reshape(1, -1),
        "ln1_g": f(p["ln1_g"][0]).reshape(1, -1), "ln1_b": f(p["ln1_b"][0]).reshape(1, -1),
        "w_ff1": f(p["w_ff1"][0]), "b_ff1": f(p["b_ff1"][0]).reshape(1, -1),
        "w_ff2": f(p["w_ff2"][0]), "b_ff2": f(p["b_ff2"][0]).reshape(1, -1),
        "ln2_g": f(p["ln2_g"][0]).reshape(1, -1), "ln2_b": f(p["ln2_b"][0]).reshape(1, -1),
        "ln_in_g": f(p["ln_in_g"]).reshape(1, -1), "ln_in_b": f(p["ln_in_b"]).reshape(1, -1),
    }
    xp = f(p["x_prompt"]); xs = f(p["x_sample"])
    maps = []
    for c in range(8):
        if c < 4:
            x = xp[c]
            hc = _host_consts_cached(4096, 1)
        else:
            x = xs[2 * (c - 4):2 * (c - 4) + 2].reshape(T, D)
            hc = _host_consts_cached(2048, 2)
        m = dict(shared)
        m["x"] = np.ascontiguousarray(x)
        m.update(hc)
        maps.append(m)
    return maps


_NC = {}


def kernel(**inputs):
    maps = make_in_maps(inputs)
    if "nc" not in _NC:
        _NC["nc"] = build_program()
    res = run_bass_kernel_spmd(_NC["nc"], maps, core_ids=list(range(8)))
    ys = [np.asarray(r["y"], dtype=np.float32) for r in res.results]
    y_prompt = np.stack(ys[0:4], axis=0)
    y_sample = np.concatenate([ys[c].reshape(2, 2048, D) for c in range(4, 8)], axis=0)
    return (y_prompt, y_sample)
```
